# Optimizing a Trainium2 kernel written in Bass

```python
import jax, jax.numpy as jnp
from jax import lax
import numpy as np


D_MODEL = 1024
BATCH = 2
SEQ = 8192
DEPTH = 4
DEC_BATCH = 8
DEC_SEQ = 32
PAST_LEN = 4096

CHUNK = 64
WINDOW = 128
N_HEADS = 8
N_KV_HEADS = 2
HEAD_DIM = 64
ATT_W = N_HEADS * HEAD_DIM
KV_W = N_KV_HEADS * HEAD_DIM
CONV_W = 512
CONV_K = 3
X_HEADS = 4
X_HEAD_DIM = 128
X_W = X_HEADS * X_HEAD_DIM
N_MEM = 256
N_BRANCH = 3
EPS = 1e-6
SPLIT_SIZES = (ATT_W, KV_W, KV_W, ATT_W, CONV_W, CONV_W, CONV_W, CONV_W, X_W, X_W, N_BRANCH * D_MODEL)
IN_W = sum(SPLIT_SIZES)
SPLIT_POINTS = tuple(int(p) for p in np.cumsum(SPLIT_SIZES)[:-1])

kernel_name = 'hybrid_swa_sink_shortconv_memxattn_stream_step'


def rmsnorm(x, g):
    x32 = x.astype(jnp.float32)
    y = x32 * lax.rsqrt(jnp.mean(x32 * x32, axis=-1, keepdims=True) + EPS)
    return (y * g.astype(jnp.float32)).astype(x.dtype)


def alibi_slopes(n):
    return jnp.power(2.0, -8.0 * jnp.arange(1, n + 1, dtype=jnp.float32) / n)


def band_attention(q, kb, vb, qpos, kpos, valid, sink):
    B, N, Q, H, D = q.shape
    G = H // N_KV_HEADS
    qg = q.reshape(B, N, Q, N_KV_HEADS, G, D)
    s = jnp.einsum('bnqkgd,bnskd->bnkgqs', qg, kb).astype(jnp.float32) * (D ** -0.5)
    dist = jnp.abs(qpos[:, :, None] - kpos[:, None, :]).astype(jnp.float32)
    slope = alibi_slopes(H).reshape(N_KV_HEADS, G)[None, None, :, :, None, None]
    s = s - slope * dist[None, :, None, None, :, :]
    s = jnp.where(valid[None, :, None, None, None, :], s, -jnp.inf)
    sink_b = sink.astype(jnp.float32).reshape(N_KV_HEADS, G)[None, None, :, :, None, None]
    m = jnp.maximum(jnp.max(s, axis=-1, keepdims=True), sink_b)
    p = jnp.exp(s - m)
    p = p / (jnp.sum(p, axis=-1, keepdims=True) + jnp.exp(sink_b - m))
    o = jnp.einsum('bnkgqs,bnskd->bnqkgd', p.astype(vb.dtype), vb)
    return o.reshape(B, N, Q, H * D)


def prompt_window_attention(q, k, v, sink):
    B, S = q.shape[:2]
    nc = S // CHUNK
    lb = WINDOW // CHUNK
    pad = ((0, 0), (WINDOW, 0), (0, 0), (0, 0))
    kp = jnp.pad(k, pad).reshape(B, nc + lb, CHUNK, N_KV_HEADS, HEAD_DIM)
    vp = jnp.pad(v, pad).reshape(B, nc + lb, CHUNK, N_KV_HEADS, HEAD_DIM)
    kb = jnp.concatenate([kp[:, j:j + nc] for j in range(lb + 1)], axis=2)
    vb = jnp.concatenate([vp[:, j:j + nc] for j in range(lb + 1)], axis=2)
    qb = q.reshape(B, nc, CHUNK, N_HEADS, HEAD_DIM)
    c0 = jnp.arange(nc)[:, None] * CHUNK
    qpos = c0 + jnp.arange(CHUNK)[None, :]
    kpos = c0 - WINDOW + jnp.arange((lb + 1) * CHUNK)[None, :]
    o = band_attention(qb, kb, vb, qpos, kpos, kpos >= 0, sink)
    return o.reshape(B, S, ATT_W)


def sample_window_attention(q, k, v, ck, cv, sink):
    B, T = q.shape[:2]
    W = ck.shape[1]
    kb = jnp.concatenate([ck, k], axis=1)[:, None]
    vb = jnp.concatenate([cv, v], axis=1)[:, None]
    qpos = (PAST_LEN + jnp.arange(T))[None, :]
    kpos = (PAST_LEN - W + jnp.arange(W + T))[None, :]
    valid = jnp.ones((1, W + T), dtype=bool)
    o = band_attention(q[:, None], kb, vb, qpos, kpos, valid, sink)
    return o.reshape(B, T, ATT_W)


def conv3(up, w, b):
    T = up.shape[1] - (CONV_K - 1)
    return sum(up[:, j:j + T] * w[j] for j in range(CONV_K)) + b


def memory_kv(mem, g, wk, wv):
    B, M = mem.shape[:2]
    mn = rmsnorm(mem, g)
    mk = (mn @ wk).reshape(B, M, X_HEADS, X_HEAD_DIM)
    mv = (mn @ wv).reshape(B, M, X_HEADS, X_HEAD_DIM)
    return mk, mv


def cross_attention(q, mk, mv):
    B, T = q.shape[:2]
    s = jnp.einsum('bthd,bmhd->bhtm', q, mk).astype(jnp.float32) * (X_HEAD_DIM ** -0.5)
    p = jax.nn.softmax(s, axis=-1).astype(mv.dtype)
    return jnp.einsum('bhtm,bmhd->bthd', p, mv).reshape(B, T, X_W)


def split_inputs(x, g, w):
    return jnp.split(rmsnorm(x, g) @ w, SPLIT_POINTS, axis=-1)


def merge_branches(attn, ga, conv, bb, gb, xattn, gx, mg, w_pa, w_pb, w_px, w_out):
    ya = (attn * jax.nn.silu(ga)) @ w_pa
    yb = (bb * conv * jax.nn.silu(gb)) @ w_pb
    yx = (xattn * jax.nn.silu(gx)) @ w_px
    sa, sb, sx = jnp.split(jax.nn.sigmoid(mg), N_BRANCH, axis=-1)
    return (sa * ya + sb * yb + sx * yx) @ w_out


def setup_inputs(seed: int = 0) -> dict:
    key = jax.random.key(seed)
    ks = jax.random.split(key, 24)
    f32 = jnp.float32
    win = min(WINDOW, PAST_LEN)
    nrm = lambda k, shape, s: (jax.random.normal(k, shape, f32) * s).astype(f32)
    return {
        'x_prompt': nrm(ks[0], (BATCH, SEQ, D_MODEL), 1.0),
        'x_sample': nrm(ks[1], (DEC_BATCH, DEC_SEQ, D_MODEL), 1.0),
        'cache_attn_k': nrm(ks[2], (DEPTH, DEC_BATCH, win, N_KV_HEADS, HEAD_DIM), 1.0),
        'cache_attn_v': nrm(ks[3], (DEPTH, DEC_BATCH, win, N_KV_HEADS, HEAD_DIM), 1.0),
        'cache_conv': nrm(ks[4], (DEPTH, DEC_BATCH, CONV_K - 1, CONV_W), 1.0),
        'cache_mem_k': nrm(ks[5], (DEPTH, DEC_BATCH, N_MEM, X_HEADS, X_HEAD_DIM), 1.0),
        'cache_mem_v': nrm(ks[6], (DEPTH, DEC_BATCH, N_MEM, X_HEADS, X_HEAD_DIM), 1.0),
        'mem_prompt': nrm(ks[7], (BATCH, N_MEM, D_MODEL), 1.0),
        'norm_g': 1.0 + nrm(ks[8], (DEPTH, D_MODEL), 0.02),
        'w_in': nrm(ks[9], (DEPTH, D_MODEL, IN_W), D_MODEL ** -0.5),
        'attn_sink': nrm(ks[10], (DEPTH, N_HEADS), 0.5),
        'w_pa': nrm(ks[11], (DEPTH, ATT_W, D_MODEL), ATT_W ** -0.5),
        'conv_w': nrm(ks[12], (DEPTH, CONV_K, CONV_W), CONV_K ** -0.5),
        'conv_b': nrm(ks[13], (DEPTH, CONV_W), 0.01),
        'w_pb': nrm(ks[14], (DEPTH, CONV_W, D_MODEL), CONV_W ** -0.5),
        'mem_norm_g': 1.0 + nrm(ks[15], (DEPTH, D_MODEL), 0.02),
        'w_mk': nrm(ks[16], (DEPTH, D_MODEL, X_W), D_MODEL ** -0.5),
        'w_mv': nrm(ks[17], (DEPTH, D_MODEL, X_W), D_MODEL ** -0.5),
        'w_px': nrm(ks[18], (DEPTH, X_W, D_MODEL), X_W ** -0.5),
        'w_out': nrm(ks[19], (DEPTH, D_MODEL, D_MODEL), D_MODEL ** -0.5),
        'final_g': 1.0 + nrm(ks[20], (D_MODEL,), 0.02),
    }


def reference(x_prompt, x_sample, cache_attn_k, cache_attn_v, cache_conv, cache_mem_k, cache_mem_v,
              mem_prompt, norm_g, w_in, attn_sink, w_pa, conv_w, conv_b, w_pb, mem_norm_g,
              w_mk, w_mv, w_px, w_out, final_g):
    xp = x_prompt
    xs = x_sample
    Bp, S = xp.shape[:2]
    Bs, T = xs.shape[:2]
    win_p = min(WINDOW, S)
    pk, pv, pc, pmk, pmv = [], [], [], [], []
    sk, sv, sc = [], [], []
    for l in range(DEPTH):
        q, k, v, ga, bb, cc, hh, gb, xq, gx, mg = split_inputs(xp, norm_g[l], w_in[l])
        q = q.reshape(Bp, S, N_HEADS, HEAD_DIM)
        k = k.reshape(Bp, S, N_KV_HEADS, HEAD_DIM)
        v = v.reshape(Bp, S, N_KV_HEADS, HEAD_DIM)
        attn = prompt_window_attention(q, k, v, attn_sink[l])
        up = jnp.pad(cc * hh, ((0, 0), (CONV_K - 1, 0), (0, 0)))
        conv = conv3(up, conv_w[l], conv_b[l])
        mk, mv = memory_kv(mem_prompt, mem_norm_g[l], w_mk[l], w_mv[l])
        xattn = cross_attention(xq.reshape(Bp, S, X_HEADS, X_HEAD_DIM), mk, mv)
        xp = xp + merge_branches(attn, ga, conv, bb, gb, xattn, gx, mg, w_pa[l], w_pb[l], w_px[l], w_out[l])
        pk.append(k[:, S - win_p:])
        pv.append(v[:, S - win_p:])
        pc.append(up[:, up.shape[1] - (CONV_K - 1):])
        pmk.append(mk)
        pmv.append(mv)
        q, k, v, ga, bb, cc, hh, gb, xq, gx, mg = split_inputs(xs, norm_g[l], w_in[l])
        q = q.reshape(Bs, T, N_HEADS, HEAD_DIM)
        k = k.reshape(Bs, T, N_KV_HEADS, HEAD_DIM)
        v = v.reshape(Bs, T, N_KV_HEADS, HEAD_DIM)
        attn = sample_window_attention(q, k, v, cache_attn_k[l], cache_attn_v[l], attn_sink[l])
        up = jnp.concatenate([cache_conv[l].astype(cc.dtype), cc * hh], axis=1)
        conv = conv3(up, conv_w[l], conv_b[l])
        xattn = cross_attention(xq.reshape(Bs, T, X_HEADS, X_HEAD_DIM), cache_mem_k[l], cache_mem_v[l])
        xs = xs + merge_branches(attn, ga, conv, bb, gb, xattn, gx, mg, w_pa[l], w_pb[l], w_px[l], w_out[l])
        sk.append(k)
        sv.append(v)
        sc.append(up[:, up.shape[1] - (CONV_K - 1):])
    y_prompt = rmsnorm(xp, final_g)
    y_sample = rmsnorm(xs, final_g)
    return (y_prompt, y_sample,
            jnp.stack(pk), jnp.stack(pv), jnp.stack(pc), jnp.stack(pmk), jnp.stack(pmv),
            jnp.stack(sk), jnp.stack(sv), jnp.stack(sc))
```

```python
import bisect
from contextlib import ExitStack

import numpy as np
import ml_dtypes

import concourse.bass as bass
import concourse.mybir as mybir
from concourse.bass_utils import run_bass_kernel_spmd

F32 = mybir.dt.float32
BF16 = mybir.dt.bfloat16
AF = mybir.ActivationFunctionType
ALU = mybir.AluOpType
AX = mybir.AxisListType

DEPTH = 4
D = 1024
NCORE = 8
OWN = 2048
HALO = 512
NP = HALO + OWN
TS = 32
NTOK = NP + TS
NCH = 22
WR_SLOTS = 3
EPS = 1e-6
NEG = -30000.0
RIDER = True


class Buf:
    __slots__ = ("name", "w", "r", "const", "excl")

    def __init__(self, name, excl=False):
        self.name = name
        self.w = None
        self.r = []
        self.const = False
        self.excl = excl


class _Op:
    __slots__ = ("fn", "waits", "inc", "dma_inc")

    def __init__(self, fn):
        self.fn = fn
        self.waits = []
        self.inc = False
        self.dma_inc = None


class _Eng:
    def __init__(self, name):
        self.name = name
        self.ops = []
        self.inc_idx = []
        self.count = 0
        self.waited = {}


class Plan:
    ENGS = ("pe", "act", "dve", "pool", "sp")

    def __init__(self):
        self.e = {n: _Eng(n) for n in self.ENGS}
        self.dma_count = {}

    def _resolve(self, tok):
        if tok[0] == "d":
            return tok[1], tok[2]
        eng = self.e[tok[1]]
        n = tok[2]
        k = bisect.bisect_left(eng.inc_idx, n)
        if k < len(eng.inc_idx):
            return "E_" + eng.name, k + 1
        eng.ops[n].inc = True
        eng.inc_idx.append(n)
        eng.count += 1
        return "E_" + eng.name, eng.count

    def emit(self, eng, fn, reads=(), writes=(), dma=None, extra=()):
        E = self.e[eng]
        deps = [t for t in extra if t is not None]
        for b in reads:
            if b.w is not None:
                deps.append(b.w)
            if b.excl:
                deps.extend(d for d in b.r if not (d[0] == "c" and d[1] == eng))
        for b in writes:
            if b.w is not None:
                deps.append(b.w)
            deps.extend(b.r)
        op = _Op(fn)
        for d in deps:
            if dma is None and d[0] == "c" and d[1] == eng and eng == "pe":
                continue
            sem, val = self._resolve(d)
            if E.waited.get(sem, 0) < val:
                E.waited[sem] = val
                op.waits.append((sem, val))
        idx = len(E.ops)
        E.ops.append(op)
        if dma is None:
            tok = ("c", eng, idx)
        else:
            c = self.dma_count.get(dma, 0) + 1
            self.dma_count[dma] = c
            op.dma_inc = dma
            tok = ("d", dma, 16 * c)
        for b in reads:
            if not b.const:
                b.r.append(tok)
        for b in writes:
            b.w = tok
            b.r = []
        return tok

    def wait_all(self, eng, toks):
        E = self.e[eng]
        op = _Op(None)
        for d in toks:
            sem, val = self._resolve(d)
            if E.waited.get(sem, 0) < val:
                E.waited[sem] = val
                op.waits.append((sem, val))
        E.ops.append(op)

    def sem_names(self):
        return ["E_" + n for n in self.ENGS] + sorted(self.dma_count.keys())

    def replay(self, eng, handle, sems):
        for op in self.e[eng].ops:
            for sem, val in op.waits:
                handle.wait_ge(sems[sem], val)
            if op.fn is None:
                continue
            ins = op.fn(handle)
            if op.inc:
                ins.then_inc(sems["E_" + eng], 1)
            if op.dma_inc is not None:
                ins.then_inc(sems[op.dma_inc], 16)


class _Rec:
    def __getattr__(self, name):
        def mk(*a, **k):
            return lambda handle: getattr(handle, name)(*a, **k)
        return mk


R = _Rec()


def build_program(n_layers=DEPTH):
    nc = bass.Bass("TRN2", target_bir_lowering=False)
    L = n_layers

    def din(name, shape, dt=F32):
        return nc.dram_tensor(name, list(shape), dt, kind="ExternalInput").ap()

    def dout(name, shape, dt=F32):
        return nc.dram_tensor(name, list(shape), dt, kind="ExternalOutput").ap()

    xin = din("xin", [NTOK, D])
    wpack = din("wpack", [L, NCH, 128, 4096])
    pvec = din("pvec", [136, 128])
    sinkb = din("sinkb", [128, 32])
    kmask = din("kmask", [128, 22])
    ident_d = din("ident", [128, 128])
    identb_d = din("identb", [128, 128], BF16)
    biasp_d = din("biasp", [128, 2 * 8 * 128], BF16)
    memp = din("memp", [256, D])
    ck_d = din("ck", [L, 128, 128])
    cv_d = din("cv", [L, 128, 128])
    cconv_d = din("cconv", [L, 2, 512])
    cmk_d = din("cmk", [L, 256, 512])
    cmv_d = din("cmv", [L, 256, 512])

    y_own = dout("y_own", [OWN, D])
    y_smp = dout("y_smp", [TS, D])
    ko = dout("ko", [L, 128, 128])
    vo = dout("vo", [L, 128, 128])
    co = dout("co", [L, 2, 512])
    mko = dout("mko", [L, 256, 512])
    mvo = dout("mvo", [L, 256, 512])
    kso = dout("kso", [L, TS, 128])
    vso = dout("vso", [L, TS, 128])
    cso = dout("cso", [L, 2, 512])

    wscr = nc.dram_tensor("wscr", [L, NCH, 128, 4096], BF16, kind="Internal").ap()
    memt_scr = nc.dram_tensor("memt_scr", [128, 8 * 256], F32, kind="Internal").ap()

    P = Plan()
    es = ExitStack()
    with es:
        def sb(name, shape, dt):
            return es.enter_context(nc.sbuf_tensor(name, list(shape), dt))

        def psum(name, shape, dt):
            return es.enter_context(nc.psum_tensor(name, list(shape), dt))

        X = sb("X", [128, 8, NTOK], F32)
        XNa = sb("XNa", [128, 8, 512], BF16)
        XNb = sb("XNb", [128, 8, 512], BF16)
        XN2 = [XNa, XNb]
        MACC = sb("MACC", [128, 8, 512], BF16)
        FT = sb("FT", [128, 8, 512], F32)
        QT = sb("QT", [128, 4, 512], BF16)
        SG = sb("SG", [128, 4, 512], BF16)
        AT = sb("AT", [128, 4, 512], BF16)
        PB = sb("PB", [128, 4, 512], BF16)
        KT = sb("KT", [128, 2, 640], BF16)
        KTs = sb("KTs", [128, 2, 160], BF16)
        VA = sb("VA", [128, 5, 2, 66], BF16)
        VAs = sb("VAs", [128, 2, 2, 66], BF16)
        ATOK = sb("ATOK", [128, 3, 4, 64], F32)
        XTOK = sb("XTOK", [128, 3, 2, 128], F32)
        DEN = sb("DEN", [128, 3, 4], F32)
        BIASP = sb("BIASP", [128, 2, 8, 128], BF16)
        UJ = sb("UJ", [128, 2, 514], F32)
        ULB = sb("ULB", [128, 4, 2], F32)
        ULBs = sb("ULBs", [128, 4, 2], F32)
        MKT = sb("MKT", [128, 4, 256], BF16)
        MVA = sb("MVA", [128, 2, 4, 130], BF16)
        MKTs = sb("MKTs", [128, 4, 256], BF16)
        MVAs = sb("MVAs", [128, 2, 4, 130], BF16)
        WR = sb("WR", [128, WR_SLOTS, 4096], BF16)
        IO = sb("IO", [128, 2, 1024], F32)
        KF = sb("KF", [128, 160], F32)
        IDENT = sb("IDENT", [128, 128], F32)
        IDB = sb("IDB", [128, 128], BF16)
        ONES = sb("ONES", [128, 128], F32)
        ONESB = sb("ONESB", [128, 128], BF16)
        PV = sb("PV", [128, 136], F32)
        ESINK = sb("ESINK", [128, 32], F32)
        KM = sb("KM", [128, 22], F32)
        EPSC = sb("EPSC", [128, 1], F32)
        MN = PB[:].rearrange("p a b -> p (a b)").rearrange("p (k m) -> p k m", k=8)

        PR = [psum(f"PR{i}", [128, 512], F32) for i in range(3)]
        SP = [psum(f"SP{i}", [128, 512], F32) for i in range(4)]
        OP = psum("OP", [128, 512], F32)

        bX = [Buf(f"X{t}") for t in range(6)]
        bXN2 = [Buf("XNa"), Buf("XNb")]
        bMACC = [Buf(f"MACC{i}") for i in range(8)]
        bFT = [Buf(f"FT{i}") for i in range(8)]
        bQT = [Buf(f"QT{i}") for i in range(4)]
        bSG = [Buf(f"SG{i}") for i in range(4)]
        bAT = [Buf(f"AT{i}") for i in range(4)]
        bPB = [Buf(f"PB{i}") for i in range(4)]
        bKT, bKTs, bVA, bVAs = Buf("KT"), Buf("KTs"), Buf("VA"), Buf("VAs")
        bATOK2 = [Buf("ATOK0"), Buf("ATOK1"), Buf("ATOK2")]
        bXTOK2 = [Buf("XTOK0"), Buf("XTOK1"), Buf("XTOK2")]
        bDEN2 = [Buf("DEN0"), Buf("DEN1"), Buf("DEN2")]
        bDEN = bDEN2[0]
        bUJ = [Buf("UJ0"), Buf("UJ1")]
        bULB, bULBs = Buf("ULB"), Buf("ULBs")
        bMKT, bMVA, bMKTs, bMVAs = Buf("MKT"), Buf("MVA"), Buf("MKTs"), Buf("MVAs")
        bWR = [Buf(f"WR{i}") for i in range(WR_SLOTS)]
        bIO = [Buf("IO0"), Buf("IO1")]
        bKF = Buf("KF")
        bC = Buf("CONST")
        bPR = [Buf(f"PR{i}", excl=True) for i in range(3)]
        bSP = [Buf(f"SP{i}", excl=True) for i in range(4)]
        bOP = Buf("OP", excl=True)
        bWC = {}
        bWCS = [Buf(f"WCS{c}") for c in range(NCH)]
        bMEMT = Buf("MEMT")

        def Aop(fn, r=(), w=()):
            return P.emit("act", fn, r, w)

        def Vop(fn, r=(), w=()):
            return P.emit("dve", fn, r, w)

        def Gop(fn, r=(), w=()):
            return P.emit("pool", fn, r, w)

        def Top(fn, r=(), w=()):
            return P.emit("pe", fn, r, w)

        def Dma(fn, r, w, sem, eng="sp"):
            return P.emit(eng, fn, r, w, dma=sem)

        pr_i = [0]
        pr_wide = [False]

        def pr_next():
            n = 8 if pr_wide[0] else 4
            k = pr_i[0] % n
            pr_i[0] += 1
            if k < 3:
                return PR[k], bPR[k]
            if k == 3:
                return OP, bOP
            return SP[k - 4], bSP[k - 4]

        ft_i = [0]

        def ft_next():
            k = ft_i[0] % 8
            ft_i[0] += 1
            return FT[:, k, :], bFT[k]

        io_i = [0]

        def io_next():
            k = io_i[0] % 2
            io_i[0] += 1
            return IO[:, k, :], bIO[k]

        def tiles_of(l):
            return [0, 1, 2, 3, 4, 5] if (l == 0 or not RIDER) else [0, 1, 2, 3, 4]

        wseq = []
        tile_seq = []
        layer_start = []
        for l in range(L):
            layer_start.append(len(wseq))
            for t in tiles_of(l):
                tile_seq.append((l, t))
                for c in range(20):
                    wseq.append((l, c))
                    if c == 12 and l == 0 and t == 0:
                        wseq.append((0, 20))
                        wseq.append((0, 21))
                    if c == 14 and t == tiles_of(l)[-1] and l + 1 < L:
                        wseq.append((l + 1, 20))
                        wseq.append((l + 1, 21))
        layer_start.append(len(wseq))
        conv_done = set()

        def emit_convert(l, c, after=()):
            if (l, c) in conv_done or l >= L:
                return
            conv_done.add((l, c))
            b = Buf(f"WC{l}_{c}")
            bWC[(l, c)] = b
            P.emit("pool", R.dma_start(out=wscr[l, c], in_=wpack[l, c]), [], [b, bWCS[c]], dma=f"D_wc{c}", extra=list(after))

        def w_load(i):
            if i >= len(wseq):
                return
            l, c = wseq[i]
            emit_convert(l, c)
            s = i % WR_SLOTS
            Dma(R.dma_start(out=WR[:, s, :], in_=wscr[l, c]), [bWC[(l, c)]], [bWR[s]], f"D_wr{s}")

        w_pos = [0]

        def w_acquire(l, c):
            i = w_pos[0]
            assert wseq[i] == (l, c), (wseq[i], l, c)
            w_pos[0] += 1
            s = i % WR_SLOTS
            return i, WR[:, s, :], bWR[s]

        nxt_order = [20, 21] + list(range(20))
        per_layer = 2 + 6 * 20

        def w_release(i):
            rd = bWR[i % WR_SLOTS].r
            tok = rd[-1] if rd else None
            w_load(i + WR_SLOTS)
            j = i + 5
            if j < 24:
                emit_convert(*wseq[j], after=[tok])
            l = max(ll for ll in range(L) if layer_start[ll] <= i) if i >= layer_start[0] else 0
            r = i - layer_start[l]
            step = max(1, (layer_start[l + 1] - layer_start[l] - 40) // NCH)
            if r >= 0 and r % step == 0 and r // step < NCH and l + 1 < L:
                emit_convert(l + 1, nxt_order[r // step], after=[tok])

        def setup():
            Dma(R.dma_start(out=IDENT[:], in_=ident_d), [], [bC], "D_c")
            Dma(R.dma_start(out=IDB[:], in_=identb_d), [], [bC], "D_c")
            Dma(R.dma_start(out=BIASP[:].rearrange("p a b c -> p (a b c)"), in_=biasp_d), [], [bC], "D_c")
            Dma(R.dma_start(out=ESINK[:], in_=sinkb), [], [bC], "D_c")
            Dma(R.dma_start(out=KM[:], in_=kmask), [], [bC], "D_c")
            Vop(R.memset(ONES[:], 1.0), [], [bC])
            Vop(R.memset(ONESB[:], 1.0), [], [bC])
            Vop(R.memset(EPSC[:], EPS), [], [bC])
            Vop(R.memset(VA[:].rearrange("p a b c -> p (a b c)"), 1.0), [], [bVA])
            Vop(R.memset(VAs[:].rearrange("p a b c -> p (a b c)"), 1.0), [], [bVAs])
            Vop(R.memset(MVA[:].rearrange("p a b c -> p (a b c)"), 1.0), [], [bMVA])
            Vop(R.memset(MVAs[:].rearrange("p a b c -> p (a b c)"), 1.0), [], [bMVAs])
            Aop(R.activation(out=ESINK[:], in_=ESINK[:], func=AF.Exp), [bC], [bC])
            io, bio = io_next()
            Dma(R.dma_start(out=io[:, 0:128], in_=pvec[0:128, :]), [], [bio], "D_" + bio.name)
            Dma(R.dma_start(out=io[0:8, 128:256], in_=pvec[128:136, :]), [], [bio], "D_" + bio.name)
            ps, bps = pr_next()
            Top(R.transpose(ps[:, 0:128], io[:, 0:128], IDENT[:]), [bio, bC], [bps])
            Top(R.transpose(ps[:, 128:136], io[0:8, 128:256], IDENT[0:8, 0:8]), [bio, bC], [bps])
            Vop(R.tensor_copy(out=PV[:], in_=ps[:, 0:136]), [bps], [bC])
            for (l, c) in wseq[:5]:
                emit_convert(l, c)
            for i in range(WR_SLOTS):
                w_load(i)
            for rb in range(21):
                n = 128 if rb < 20 else TS
                r0 = rb * 128
                io, bio = io_next()
                Dma(R.dma_start(out=io[0:n, :], in_=xin[r0:r0 + n, :]), [], [bio], "D_" + bio.name)
                bx = bX[min(rb // 4, 5)]
                for hf in range(2):
                    ps, bps = pr_next()
                    for k4 in range(4):
                        kb = hf * 4 + k4
                        Top(R.transpose(ps[:, k4 * 128:k4 * 128 + n], io[0:n, kb * 128:(kb + 1) * 128], IDENT[0:n, 0:n]),
                            [bio, bC], [bps])
                    src = ps[:].rearrange("p (k c) -> p k c", k=4)[:, :, 0:n]
                    dst = X[:, hf * 4:hf * 4 + 4, r0:r0 + n]
                    if hf == 0:
                        Vop(R.tensor_copy(out=dst, in_=src), [bps], [bx])
                    else:
                        Aop(R.activation(out=dst, in_=src, func=AF.Copy), [bps], [bx])
        def setup_mem():
            for mb in range(2):
                io, bio = io_next()
                io2, bio2 = io_next()
                Dma(R.dma_start(out=io, in_=memp[mb * 128:(mb + 1) * 128, :]), [], [bio], "D_" + bio.name)
                Aop(R.activation(out=io2, in_=io, func=AF.Square), [bio], [bio2])
                Vop(R.reduce_sum(out=DEN[:, 0, 0:1], in_=io2, axis=AX.X), [bio2], [bDEN])
                Aop(R.activation(out=DEN[:, 0, 0:1], in_=DEN[:, 0, 0:1], func=AF.Sqrt, scale=1.0 / D, bias=EPSC[:, 0:1]), [bDEN, bC], [bDEN])
                Vop(R.reciprocal(out=DEN[:, 0, 1:2], in_=DEN[:, 0, 0:1]), [bDEN], [bDEN])
                Vop(R.tensor_scalar(out=io, in0=io, scalar1=DEN[:, 0, 1:2], scalar2=None, op0=ALU.mult), [bio, bDEN], [bio])
                for hf in range(2):
                    ps, bps = pr_next()
                    for k4 in range(4):
                        kb = hf * 4 + k4
                        Top(R.transpose(ps[:, k4 * 128:(k4 + 1) * 128], io[:, kb * 128:(kb + 1) * 128], IDENT[:]),
                            [bio, bC], [bps])
                    Vop(R.tensor_copy(out=io2[:, hf * 512:(hf + 1) * 512], in_=ps[:]), [bps], [bio2])
                dst = memt_scr.rearrange("p (k m) -> p k m", k=8)[:, :, mb * 128:(mb + 1) * 128]
                Dma(R.dma_start(out=dst, in_=io2.rearrange("p (k m) -> p k m", k=8)), [bio2], [bMEMT], "D_" + bio2.name)

        def proj(wt, bw, lhs_of_kb, rhs, brhs, T, nkb=8):
            ps, bps = pr_next()
            for kb in range(nkb):
                Top(R.matmul(ps[:, 0:T], lhsT=lhs_of_kb(kb), rhs=rhs(kb), start=(kb == 0), stop=(kb == nkb - 1)),
                    [bw] + list(brhs), [bps])
            return ps, bps

        def w8(wt, cb):
            v = wt.rearrange("p (k c) -> p k c", k=8)
            return lambda kb: v[:, kb, cb * 128:(cb + 1) * 128]

        def w4(wt, ob):
            v = wt.rearrange("p (k c) -> p k c", k=4)
            return lambda kb: v[:, kb, ob * 128:(ob + 1) * 128]

        def tanh_gate(ps, bps, T, out_ap, bout):
            tg, btg = ft_next()
            Aop(R.activation(out=tg[:, 0:T], in_=ps[:, 0:T], func=AF.Tanh, scale=0.5), [bps], [btg])
            Vop(R.scalar_tensor_tensor(out=out_ap, in0=tg[:, 0:T], scalar=1.0, in1=ps[:, 0:T], op0=ALU.add, op1=ALU.mult), [btg, bps], [bout])

        def merge_branch(l, cm0, cm1, cp, T, first, last, XN, bXN):
            i0, wm0, bwm0 = w_acquire(l, cm0)
            i1, wm1, bwm1 = w_acquire(l, cm1)
            ip, wp, bwp = w_acquire(l, cp)

            def gate(ob):
                wm, bwm = (wm0, bwm0) if ob < 4 else (wm1, bwm1)
                ps, bps = proj(wm, bwm, w8(wm, ob % 4), lambda kb: XN[:, kb, 0:T], [bXN], T)
                tm, btm = ft_next()
                Aop(R.activation(out=tm[:, 0:T], in_=ps[:, 0:T], func=AF.Tanh, scale=0.5), [bps], [btm])
                if ob == 3:
                    w_release(i0)
                return tm, btm

            pend = gate(0)
            for ob in range(8):
                tm, btm = pend
                if ob + 1 < 8:
                    pend = gate(ob + 1)
                ps2, bps2 = proj(wp, bwp, w4(wp, ob), lambda kb: AT[:, kb, 0:T], bAT, T, nkb=4)
                if first:
                    Vop(R.scalar_tensor_tensor(out=MACC[:, ob, 0:T], in0=tm[:, 0:T], scalar=1.0, in1=ps2[:, 0:T],
                                               op0=ALU.add, op1=ALU.mult), [btm, bps2], [bMACC[ob]])
                else:
                    Vop(R.scalar_tensor_tensor(out=tm[:, 0:T], in0=tm[:, 0:T], scalar=1.0, in1=ps2[:, 0:T],
                                               op0=ALU.add, op1=ALU.mult), [btm, bps2], [btm])
                    Vop(R.tensor_tensor(out=MACC[:, ob, 0:T], in0=MACC[:, ob, 0:T], in1=tm[:, 0:T], op=ALU.add),
                        [btm, bMACC[ob]], [bMACC[ob]])
            w_release(i1)
            w_release(ip)

        def out_rows(src_ps, bps, n, ncols, dst):
            io, bio = io_next()
            Aop(R.activation(out=io[0:n, 0:ncols], in_=src_ps, func=AF.Copy), [bps], [bio])
            Dma(R.dma_start(out=dst, in_=io[0:n, 0:ncols]), [bio], [], "D_" + bio.name)

        import os as _os2
        _sub = int(_os2.environ.get("MK_SUB", "99"))

        def layer_setup(l):
            mgcol = l * 32 + 8
            def stage(fn_list):
                ap, b = ft_next()
                for mk in fn_list:
                    Dma(mk(ap), [], [b], "D_" + b.name)
                return ap, b
            ckv = ck_d[l].rearrange("t (k d) -> t k d", k=2)
            s_ck = stage([lambda ap, dup=dup: R.dma_start(out=ap[:, 0:256].rearrange("p (k u d) -> p k u d", k=2, u=2)[:, :, dup, :], in_=ckv)
                          for dup in range(2)])
            s_cv = stage([lambda ap: R.dma_start(out=ap[:, 0:128], in_=cv_d[l])])
            s_cc = stage([lambda ap: R.dma_start(out=ap[0:2, 0:512], in_=cconv_d[l])])
            s_mk = [stage([lambda ap, mb=mb: R.dma_start(out=ap[:, 0:512], in_=cmk_d[l, mb * 128:(mb + 1) * 128, :])]) for mb in range(2)]
            s_mv = [stage([lambda ap, mb=mb: R.dma_start(out=ap[:, 0:512], in_=cmv_d[l, mb * 128:(mb + 1) * 128, :])]) for mb in range(2)]
            for hf in range(2):
                io, bio = io_next()
                Dma(R.dma_start(out=io, in_=memt_scr[:, hf * 1024:(hf + 1) * 1024]), [bMEMT], [bio], "D_" + bio.name)
                for k4 in range(4):
                    kb = hf * 4 + k4
                    Vop(R.tensor_scalar(out=MN[:, kb, :], in0=io[:, k4 * 256:(k4 + 1) * 256], scalar1=PV[:, mgcol + kb:mgcol + kb + 1],
                                                                   scalar2=None, op0=ALU.mult), [bio, bC], bPB)
            if _sub < 1:
                return
            ik, wk, bwk = w_acquire(l, 20)
            iv, wv, bwv = w_acquire(l, 21)
            wk8 = wk.rearrange("p (k c) -> p k c", k=8)
            wv8 = wv.rearrange("p (k c) -> p k c", k=8)
            for mb in range(2):
                ps, bps = proj(wk, bwk, lambda kb, mb=mb: MN[:, kb, mb * 128:(mb + 1) * 128], lambda kb: wk8[:, kb, :], bPB, 512)
                out_rows(ps[:, 0:512], bps, 128, 512, mko[l, mb * 128:(mb + 1) * 128, :])
            if _sub < 2:
                return
            for h in range(4):
                ps, bps = proj(wk, bwk, lambda kb, h=h: wk8[:, kb, h * 128:(h + 1) * 128], lambda kb: MN[:, kb, :], bPB, 256)
                Vop(R.tensor_copy(out=MKT[:, h, :], in_=ps[:, 0:256]), [bps], [bMKT])
            if _sub < 3:
                return
            for mb in range(2):
                ps, bps = proj(wv, bwv, lambda kb, mb=mb: MN[:, kb, mb * 128:(mb + 1) * 128], lambda kb: wv8[:, kb, :], bPB, 512)
                if _os2.environ.get("MK_X1", "0") != "1":
                    Vop(R.tensor_copy(out=MVA[:, mb, :, 0:128], in_=ps[:].rearrange("p (h d) -> p h d", h=4)), [bps], [bMVA])
                out_rows(ps[:, 0:512], bps, 128, 512, mvo[l, mb * 128:(mb + 1) * 128, :])
            if _os2.environ.get("MK_X2", "0") != "1":
                w_release(ik)
                w_release(iv)
            if _sub < 4:
                return
            io, bio = s_ck
            for kv in range(2):
                ps, bps = pr_next()
                Top(R.transpose(ps[:, 0:128], io[:, kv * 128:(kv + 1) * 128], IDENT[:]), [bio, bC], [bps])
                Vop(R.tensor_copy(out=KTs[:, kv, 0:128], in_=ps[:, 0:128]), [bps], [bKTs])
            io, bio = s_cv
            Vop(R.tensor_copy(out=VAs[:, 0, :, 0:64], in_=io[:, 0:128].rearrange("p (k d) -> p k d", k=2)), [bio], [bVAs])
            io, bio = s_cc
            ps, bps = pr_next()
            for j in range(4):
                Top(R.transpose(ps[:, j * 2:j * 2 + 2], io[0:2, j * 128:(j + 1) * 128], IDENT[0:2, 0:2]), [bio, bC], [bps])
            Aop(R.activation(out=ULBs[:].rearrange("p a b -> p (a b)"), in_=ps[:, 0:8], func=AF.Copy), [bps], [bULBs])
            for mb in range(2):
                io, bio = s_mk[mb]
                ps, bps = pr_next()
                for h in range(4):
                    Top(R.transpose(ps[:, h * 128:(h + 1) * 128], io[:, h * 128:(h + 1) * 128], IDENT[:]), [bio, bC], [bps])
                Vop(R.tensor_copy(out=MKTs[:, :, mb * 128:(mb + 1) * 128], in_=ps[:].rearrange("p (h m) -> p h m", h=4)), [bps], [bMKTs])
                io, bio = s_mv[mb]
                Vop(R.tensor_copy(out=MVAs[:, mb, :, 0:128], in_=io[:, 0:512].rearrange("p (h d) -> p h d", h=4)), [bio], [bMVAs])

        def tparams(l, t):
            if t == 5:
                return 0, 0, TS
            if t == 0:
                return 512 - 128 * l, 128 * l, (TS if (RIDER and l >= 1) else 0)
            return 512, 512 * t, 0

        def tsegs(l, t):
            T, c0, E = tparams(l, t)
            segs = []
            if T:
                segs.append((0, T, c0, bX[t]))
            if E:
                segs.append((T, E, NP, bX[5]))
            return T, c0, E, segs

        tile_ctr = [0]
        norm_pending = {}
        norm_now = set()

        sq_pending = {}

        def norm_part1a(l, t, k):
            T, c0, E, segs = tsegs(l, t)
            sqs = []
            for pair in range(4):
                sq, bsq = ft_next()
                sqb = sq.bitcast(BF16)
                for half in range(2):
                    kb = 2 * pair + half
                    for (lo, n, xc, bx) in segs:
                        Aop(R.activation(out=sqb[:, half * 512 + lo:half * 512 + lo + n], in_=X[:, kb, xc:xc + n], func=AF.Square), [bx], [bsq])
                    sqs.append((sqb[:, half * 512:(half + 1) * 512], bsq))
            sq_pending[k] = sqs

        def norm_part1(l, t, k):
            T, c0, E, segs = tsegs(l, t)
            W = T + E
            if k not in sq_pending:
                norm_part1a(l, t, k)
            sqs = sq_pending.pop(k)
            ps, bps = pr_next()
            for kb in range(8):
                sq, bsq = sqs[kb]
                Top(R.matmul(ps[:, 0:W], lhsT=ONESB[:], rhs=sq[:, 0:W], start=(kb == 0), stop=(kb == 7)), [bsq, bC], [bps])
            ms, bms = ft_next()
            Aop(R.activation(out=ms[:, 0:W], in_=ps[:, 0:W], func=AF.Sqrt, scale=1.0 / D, bias=EPSC[:, 0:1]), [bps, bC], [bms])
            Vop(R.reciprocal(out=ms[:, 0:W], in_=ms[:, 0:W]), [bms], [bms])
            norm_pending[k] = (ms, bms)

        def norm_part2(l, t, k):
            T, c0, E, segs = tsegs(l, t)
            XN, bXN = XN2[k % 2], bXN2[k % 2]
            ms, bms = norm_pending.pop(k)
            gcol = l * 32
            for kb in range(8):
                for (lo, n, xc, bx) in segs:
                    Vop(R.scalar_tensor_tensor(out=XN[:, kb, lo:lo + n], in0=X[:, kb, xc:xc + n], scalar=PV[:, gcol + kb:gcol + kb + 1],
                                               in1=ms[:, lo:lo + n], op0=ALU.mult, op1=ALU.mult), [bx, bms, bC], [bXN])

        def tile_layer(l, t):
            T, c0, E, segs = tsegs(l, t)
            W = T + E
            has_p, has_s = T > 0, E > 0
            k = tile_ctr[0]
            tile_ctr[0] += 1
            XN, bXN = XN2[k % 2], bXN2[k % 2]
            gcol = l * 32
            bias = BIASP

            if t == 0:
                Vop(R.memset(ULB[:].rearrange("p a b -> p (a b)"), 0.0), [], [bULB])
                Vop(R.memset(KT[:, :, 0:128], 0.0), [], [bKT])
            if k == 0 or k in norm_now:
                norm_part1(l, t, k)
                norm_part2(l, t, k)
            xr = lambda kb: XN[:, kb, 0:W]

            i0, w0, bw0 = w_acquire(l, 0)
            for j in range(4):
                ps, bps = proj(w0, bw0, w8(w0, j), xr, [bXN], W)
                Aop(R.activation(out=QT[:, j, 0:W], in_=ps[:, 0:W], func=AF.Identity, scale=0.125), [bps], [bQT[j]])
            w_release(i0)
            i1, w1, bw1 = w_acquire(l, 1)
            for kv in range(2):
                ps, bps = proj(w1, bw1, w8(w1, kv), xr, [bXN], W)
                lo = kv * 64
                if has_p:
                    Aop(R.activation(out=KT[:, kv, 128:128 + T], in_=ps[:, 0:T], func=AF.Copy), [bps], [bKT])
                if has_s:
                    Aop(R.activation(out=KTs[:, kv, 128:128 + E], in_=ps[:, T:W], func=AF.Copy), [bps], [bKTs])
                    Aop(R.activation(out=KF[lo:lo + 64, 128:160], in_=ps[lo:lo + 64, T:W], func=AF.Copy), [bps], [bKF])
                if t == 4:
                    Aop(R.activation(out=KF[lo:lo + 64, 0:128], in_=ps[lo:lo + 64, 384:512], func=AF.Copy), [bps], [bKF])
            ps, bps = proj(w1, bw1, w8(w1, 2), xr, [bXN], W)
            vt, bvt = ft_next()
            Aop(R.activation(out=vt[:, 0:W], in_=ps[:, 0:W], func=AF.Copy), [bps], [bvt])
            w_release(i1)
            if has_s:
                ps, bps = pr_next()
                Top(R.transpose(ps[0:TS, 0:128], KF[:, 128:160], IDENT[:]), [bKF, bC], [bps])
                out_rows(ps[0:TS, 0:128], bps, TS, 128, kso[l])
            if t == 4:
                ps, bps = pr_next()
                Top(R.transpose(ps[:, 0:128], KF[:, 0:128], IDENT[:]), [bKF, bC], [bps])
                out_rows(ps[:, 0:128], bps, 128, 128, ko[l])
            nvb = T // 128
            for i in range(nvb):
                ps, bps = pr_next()
                Top(R.transpose(ps[:, 0:128], vt[:, i * 128:(i + 1) * 128], IDENT[:]), [bvt, bC], [bps])
                Aop(R.activation(out=VA[:, 1 + i, :, 0:64], in_=ps[:, 0:128].rearrange("p (k d) -> p k d", k=2), func=AF.Copy), [bps], [bVA])
                if t == 4 and i == nvb - 1:
                    out_rows(ps[:, 0:128], bps, 128, 128, vo[l])
            if has_s:
                ps, bps = pr_next()
                Top(R.transpose(ps[0:TS, 0:128], vt[:, T:W], IDENT[:]), [bvt, bC], [bps])
                Aop(R.activation(out=VAs[0:TS, 1, :, 0:64], in_=ps[0:TS, 0:128].rearrange("p (k d) -> p k d", k=2), func=AF.Copy), [bps], [bVAs])
                out_rows(ps[0:TS, 0:128], bps, TS, 128, vso[l])
            nxt = tile_seq[k + 1] if k + 1 < len(tile_seq) else None
            if nxt is not None and has_s and tparams(nxt[0], nxt[1])[2] > 0:
                norm_now.add(k + 1)
                nxt = None
            if nxt is not None:
                norm_part1a(nxt[0], nxt[1], k + 1)
            i2, w2, bw2 = w_acquire(l, 2)
            for j in range(4):
                ps, bps = proj(w2, bw2, w8(w2, j), xr, [bXN], W)
                tanh_gate(ps, bps, W, SG[:, j, 0:W], bSG[j])
            w_release(i2)

            if nxt is not None:
                norm_part1(nxt[0], nxt[1], k + 1)

            units = []
            for pr in range(T // 128):
                for g in range(2):
                    units.append(dict(c=128 * pr, nq=128, kt=KT, bkt=bKT, va=VA, bva=bVA, vbA=pr, kA=128 * pr, smp=False, g=g,
                                      gx=(c0 + 128 * pr) // 128))
            if has_s:
                for g in range(2):
                    units.append(dict(c=T, nq=TS, kt=KTs, bkt=bKTs, va=VAs, bva=bVAs, vbA=0, kA=0, smp=True, g=g, gx=0))

            def emit_qk(ui):
                u = units[ui]
                c, nq, g, kt, bkt = u["c"], u["nq"], u["g"], u["kt"], u["bkt"]
                sa, bsa = SP[2 * (ui % 2)], bSP[2 * (ui % 2)]
                sbk, bsb = SP[2 * (ui % 2) + 1], bSP[2 * (ui % 2) + 1]
                for hh in range(4):
                    h = 4 * g + hh
                    lo = 64 * (h % 2)
                    qb = h // 2
                    for blk, (sps, bsp, nk, k0) in enumerate(((sa, bsa, 128, u["kA"]), (sbk, bsb, nq, u["kA"] + 128))):
                        Top(R.matmul(sps[0:nk, hh * nq:(hh + 1) * nq], lhsT=kt[lo:lo + 64, g, k0:k0 + nk], rhs=QT[lo:lo + 64, qb, c:c + nq],
                                     start=True, stop=False), [bkt, bQT[qb]], [bsp])
                        Top(R.matmul(sps[0:nk, hh * nq:(hh + 1) * nq], lhsT=bias[0:nq, blk, h, 0:nk], rhs=IDB[0:nq, 0:nq],
                                     start=False, stop=True), [bC], [bsp])

            def emit_a(ui):
                u = units[ui]
                c, nq, g, va, bva = u["c"], u["nq"], u["g"], u["va"], u["bva"]
                s0 = 2 * (ui % 2)
                par = ui % 3
                gx = u["gx"]
                for blk, nk in ((0, 128), (1, nq)):
                    sps, bsp = SP[s0 + blk], bSP[s0 + blk]
                    if u["smp"]:
                        Aop(R.activation(out=PB[0:nk, s0 + blk, 0:4 * nq], in_=sps[0:nk, 0:4 * nq], func=AF.Exp),
                            [bsp], [bPB[s0 + blk]])
                    else:
                        Aop(R.activation(out=PB[0:nk, s0 + blk, 0:4 * nq], in_=sps[0:nk, 0:4 * nq], func=AF.Exp,
                                         bias=KM[0:nk, gx + blk:gx + blk + 1]), [bsp, bC], [bPB[s0 + blk]])
                po, bpo = pr_next()
                for hh in range(4):
                    for blk, nk in ((0, 128), (1, nq)):
                        vb = u["vbA"] + blk
                        Top(R.matmul(po[0:nq, hh * 65:(hh + 1) * 65], lhsT=PB[0:nk, s0 + blk, hh * nq:(hh + 1) * nq],
                                     rhs=va[0:nk, vb, g, 0:65], start=(blk == 0), stop=(blk == 1)),
                            [bPB[s0 + blk], bva], [bpo])
                o3 = po[:, 0:260].rearrange("p (h d) -> p h d", h=4)
                sk = l * 8 + 4 * g
                den = DEN[:, par, :]
                Vop(R.tensor_tensor(out=den[0:nq, 0:4].unsqueeze(2), in0=o3[0:nq, :, 64:65], in1=ESINK[0:nq, sk:sk + 4].unsqueeze(2), op=ALU.add),
                    [bpo, bC], [bDEN2[par]])
                Vop(R.reciprocal(out=den[0:nq, 0:4], in_=den[0:nq, 0:4]), [bDEN2[par]], [bDEN2[par]])
                Vop(R.tensor_tensor(out=ATOK[0:nq, par, :, :], in0=o3[0:nq, :, 0:64], in1=den[0:nq, 0:4].unsqueeze(2).to_broadcast([nq, 4, 64]), op=ALU.mult),
                    [bpo, bDEN2[par]], [bATOK2[par]])

            def emit_b(ui):
                u = units[ui]
                c, nq, g = u["c"], u["nq"], u["g"]
                par = ui % 3
                ps, bps = pr_next()
                for j in range(2):
                    Top(R.transpose(ps[:, j * 128:j * 128 + nq], ATOK[0:nq, par, 2 * j:2 * j + 2, :].rearrange("p a b -> p (a b)"), IDENT[0:nq, 0:nq]),
                        [bATOK2[par], bC], [bps])
                Vop(R.tensor_tensor(out=AT[:, 2 * g:2 * g + 2, c:c + nq], in0=ps[:, 0:256].rearrange("p (j q) -> p j q", j=2)[:, :, 0:nq],
                                    in1=SG[:, 2 * g:2 * g + 2, c:c + nq], op=ALU.mult),
                    [bps, bSG[2 * g], bSG[2 * g + 1]], [bAT[2 * g], bAT[2 * g + 1]])

            pr_wide[0] = False
            nun = len(units)
            emit_qk(0)
            if nun > 1:
                emit_qk(1)
            for ui in range(nun):
                emit_a(ui)
                if ui + 2 < nun:
                    emit_qk(ui + 2)
                if ui >= 2:
                    emit_b(ui - 2)
            if nun >= 2:
                emit_b(nun - 2)
            emit_b(nun - 1)
            if has_p:
                Aop(R.activation(out=KT[:, :, 0:128], in_=KT[:, :, T:T + 128], func=AF.Copy), [bKT], [bKT])
                Aop(R.activation(out=VA[:, 0, :, 0:64], in_=VA[:, T // 128, :, 0:64], func=AF.Copy), [bVA], [bVA])

            pr_wide[0] = True
            if nxt is not None:
                norm_part2(nxt[0], nxt[1], k + 1)

            merge_branch(l, 3, 4, 5, W, True, False, XN, bXN)

            cw = gcol + 16
            cbc = gcol + 28
            csegs = []
            if has_p:
                csegs.append((0, T, 0))
            if has_s:
                csegs.append((T, E, T + 2))
            for j in range(4):
                ij, wj, bwj = w_acquire(l, 6 + j)
                uj, buj = UJ[:, j % 2, :], bUJ[j % 2]
                ps, bps = proj(wj, bwj, w8(wj, 0), xr, [bXN], W)
                cc, bcc = ft_next()
                Aop(R.activation(out=cc[:, 0:W], in_=ps[:, 0:W], func=AF.Copy), [bps], [bcc])
                if has_p:
                    Aop(R.activation(out=uj[:, 0:2], in_=ULB[:, j, :], func=AF.Copy), [bULB], [buj])
                if has_s:
                    Aop(R.activation(out=uj[:, T + 2:T + 4], in_=ULBs[:, j, :], func=AF.Copy), [bULBs], [buj])
                ps, bps = proj(wj, bwj, w8(wj, 1), xr, [bXN], W)
                for (lo, n, uo) in csegs:
                    Vop(R.tensor_tensor(out=uj[:, uo + 2:uo + 2 + n], in0=cc[:, lo:lo + n], in1=ps[:, lo:lo + n], op=ALU.mult), [bcc, bps], [buj])
                cv, bcv = ft_next()
                for (lo, n, uo) in csegs:
                    Aop(R.activation(out=cv[:, lo:lo + n], in_=uj[:, uo + 2:uo + 2 + n], func=AF.Identity, scale=PV[:, cw + 8 + j:cw + 9 + j],
                                     bias=PV[:, cbc + j:cbc + j + 1]), [buj, bC], [bcv])
                for tap, pc in ((1, cw + 4 + j), (0, cw + j)):
                    for (lo, n, uo) in csegs:
                        Vop(R.scalar_tensor_tensor(out=cv[:, lo:lo + n], in0=uj[:, uo + tap:uo + tap + n], scalar=PV[:, pc:pc + 1], in1=cv[:, lo:lo + n],
                                                   op0=ALU.mult, op1=ALU.add), [buj, bcv, bC], [bcv])
                if has_p:
                    Aop(R.activation(out=ULB[:, j, :], in_=uj[:, T:T + 2], func=AF.Copy), [buj], [bULB])
                if has_s:
                    Aop(R.activation(out=ULBs[:, j, :], in_=uj[:, T + 2 + E:T + 4 + E], func=AF.Copy), [buj], [bULBs])
                ps, bps = proj(wj, bwj, w8(wj, 2), xr, [bXN], W)
                Vop(R.tensor_tensor(out=cv[:, 0:W], in0=cv[:, 0:W], in1=ps[:, 0:W], op=ALU.mult), [bcv, bps], [bcv])
                ps, bps = proj(wj, bwj, w8(wj, 3), xr, [bXN], W)
                tg, btg = ft_next()
                Aop(R.activation(out=tg[:, 0:W], in_=ps[:, 0:W], func=AF.Tanh, scale=0.5), [bps], [btg])
                Vop(R.scalar_tensor_tensor(out=tg[:, 0:W], in0=tg[:, 0:W], scalar=1.0, in1=ps[:, 0:W], op0=ALU.add, op1=ALU.mult),
                    [btg, bps], [btg])
                Vop(R.tensor_tensor(out=AT[:, j, 0:W], in0=cv[:, 0:W], in1=tg[:, 0:W], op=ALU.mult), [bcv, btg], [bAT[j]])
                w_release(ij)
            for (flag, ulb, bulb, dst) in ((has_s, ULBs, bULBs, cso[l]), (t == 4, ULB, bULB, co[l])):
                if flag:
                    ps, bps = pr_next()
                    for j in range(4):
                        Top(R.transpose(ps[0:2, j * 128:(j + 1) * 128], ulb[:, j, :], IDENT[:]), [bulb, bC], [bps])
                    out_rows(ps[0:2, 0:512], bps, 2, 512, dst)

            merge_branch(l, 10, 11, 12, W, False, False, XN, bXN)
            if l == 0 and t == 0:
                setup_mem()
                layer_setup(0)

            ix, wx, bwx = w_acquire(l, 13)
            for j in range(4):
                ps, bps = proj(wx, bwx, w8(wx, j), xr, [bXN], W)
                Aop(R.activation(out=QT[:, j, 0:W], in_=ps[:, 0:W], func=AF.Identity, scale=128.0 ** -0.5), [bps], [bQT[j]])
            w_release(ix)
            ig, wg, bwg = w_acquire(l, 14)
            for j in range(4):
                ps, bps = proj(wg, bwg, w8(wg, j), xr, [bXN], W)
                tanh_gate(ps, bps, W, SG[:, j, 0:W], bSG[j])
            w_release(ig)
            pr_wide[0] = False
            xunits = [(i * 128, 128, MVA, bMVA) for i in range(T // 128)]
            if has_s:
                xunits.append((T, TS, MVAs, bMVAs))
            xcnt = [0]
            xpend = []

            def x_scores(hp):
                for hh in range(2):
                    h = 2 * hp + hh
                    for blk in range(2):
                        kk = 2 * hh + blk
                        if has_p:
                            Top(R.matmul(SP[kk][:, 0:T], lhsT=MKT[:, h, blk * 128:(blk + 1) * 128], rhs=QT[:, h, 0:T], start=True, stop=True),
                                [bMKT, bQT[h]], [bSP[kk]])
                        if has_s:
                            Top(R.matmul(SP[kk][:, T:W], lhsT=MKTs[:, h, blk * 128:(blk + 1) * 128], rhs=QT[:, h, T:W], start=True, stop=True),
                                [bMKTs, bQT[h]], [bSP[kk]])
                        Aop(R.activation(out=PB[:, kk, 0:W], in_=SP[kk][:, 0:W], func=AF.Exp), [bSP[kk]], [bPB[kk]])

            def x_a(hp, xu):
                col, nt, mva, bmva = xu
                par = xcnt[0] % 3
                xcnt[0] += 1
                po, bpo = pr_next()
                for hh in range(2):
                    h = 2 * hp + hh
                    for blk in range(2):
                        kk = 2 * hh + blk
                        Top(R.matmul(po[0:nt, hh * 129:(hh + 1) * 129], lhsT=PB[:, kk, col:col + nt],
                                     rhs=mva[:, blk, h, 0:129], start=(blk == 0), stop=(blk == 1)), [bPB[kk], bmva], [bpo])
                o2 = po[:, 0:258].rearrange("p (h d) -> p h d", h=2)
                den = DEN[:, par, :]
                Vop(R.reciprocal(out=den[0:nt, 0:2].unsqueeze(2), in_=o2[0:nt, :, 128:129]), [bpo], [bDEN2[par]])
                Vop(R.tensor_tensor(out=XTOK[0:nt, par, :, :], in0=o2[0:nt, :, 0:128], in1=den[0:nt, 0:2].unsqueeze(2).to_broadcast([nt, 2, 128]), op=ALU.mult),
                    [bpo, bDEN2[par]], [bXTOK2[par]])
                return (hp, col, nt, par)

            def x_b(st):
                hp, col, nt, par = st
                ps, bps = pr_next()
                for hh in range(2):
                    Top(R.transpose(ps[:, hh * 128:hh * 128 + nt], XTOK[0:nt, par, hh, :], IDENT[0:nt, 0:nt]), [bXTOK2[par], bC], [bps])
                Vop(R.tensor_tensor(out=AT[:, 2 * hp:2 * hp + 2, col:col + nt], in0=ps[:, 0:256].rearrange("p (j q) -> p j q", j=2)[:, :, 0:nt],
                                    in1=SG[:, 2 * hp:2 * hp + 2, col:col + nt], op=ALU.mult),
                    [bps, bSG[2 * hp], bSG[2 * hp + 1]], [bAT[2 * hp], bAT[2 * hp + 1]])

            for hp in range(2):
                x_scores(hp)
                for xu in xunits:
                    xpend.append(x_a(hp, xu))
                    if len(xpend) > 2:
                        x_b(xpend.pop(0))
            while xpend:
                x_b(xpend.pop(0))

            pr_wide[0] = True
            if t == tiles_of(l)[-1] and l + 1 < L:
                layer_setup(l + 1)
            merge_branch(l, 15, 16, 17, W, False, True, XN, bXN)

            for half in range(2):
                io_, wo, bwo = w_acquire(l, 18 + half)
                for o4 in range(4):
                    ob = half * 4 + o4
                    ps, bps = proj(wo, bwo, w8(wo, o4), lambda kb: MACC[:, kb, 0:W], bMACC, W)
                    for (lo, n, xc, bx) in segs:
                        Vop(R.scalar_tensor_tensor(out=X[:, ob, xc:xc + n], in0=ps[:, lo:lo + n], scalar=0.25, in1=X[:, ob, xc:xc + n],
                                                   op0=ALU.mult, op1=ALU.add), [bps, bx], [bx])
                w_release(io_)

        yc = [0]

        def final_tiles(which):
            fg = 128
            pr_wide[0] = True
            tiles = []
            for q in which:
                if q == 5:
                    tiles.append((5, NP, TS, y_smp))
                else:
                    tiles.append((q, HALO + 512 * (q - 1), 512, y_own[512 * (q - 1):512 * q, :]))
            for (t, c0, T, dst) in tiles:
                ps, bps = pr_next()
                for kb in range(8):
                    sq, bsq = FT[:, 1 + kb % 2, :].bitcast(BF16), bFT[1 + kb % 2]
                    Aop(R.activation(out=sq[:, 0:T], in_=X[:, kb, c0:c0 + T], func=AF.Square), [bX[t]], [bsq])
                    Top(R.matmul(ps[:, 0:T], lhsT=ONESB[:], rhs=sq[:, 0:T], start=(kb == 0), stop=(kb == 7)), [bsq, bC], [bps])
                ms, bms = FT[:, 0, :], bFT[0]
                Aop(R.activation(out=ms[:, 0:T], in_=ps[:, 0:T], func=AF.Sqrt, scale=1.0 / D, bias=EPSC[:, 0:1]), [bps, bC], [bms])
                Vop(R.reciprocal(out=ms[:, 0:T], in_=ms[:, 0:T]), [bms], [bms])
                items = [(b0, min(128, T - b0), kb) for b0 in range(0, T, 128) for kb in range(8)]
                yts = {}
                cur = {}

                def f_s(idx):
                    b0, n, kb = items[idx]
                    sl = 1 + (yc[0] % 7)
                    yc[0] += 1
                    yt, byt = FT[:, sl, :], bFT[sl]
                    Vop(R.scalar_tensor_tensor(out=yt[:, 0:n], in0=X[:, kb, c0 + b0:c0 + b0 + n], scalar=PV[:, fg + kb:fg + kb + 1],
                                               in1=ms[:, b0:b0 + n], op0=ALU.mult, op1=ALU.mult), [bX[t], bms, bC], [byt])
                    yts[idx] = (yt, byt)

                def f_t(idx):
                    b0, n, kb = items[idx]
                    hf, k4 = divmod(kb, 4)
                    if kb == 0:
                        cur["io"] = io_next()
                    if k4 == 0:
                        cur["ps"] = pr_next()
                    io, bio = cur["io"]
                    ps2, bps2 = cur["ps"]
                    yt, byt = yts.pop(idx)
                    Top(R.transpose(ps2[0:n, k4 * 128:(k4 + 1) * 128], yt[:, 0:n], IDENT[:]), [byt, bC], [bps2])
                    if k4 == 3:
                        if hf == 0:
                            Aop(R.activation(out=io[0:n, 0:512], in_=ps2[0:n, :], func=AF.Copy), [bps2], [bio])
                        else:
                            Vop(R.tensor_copy(out=io[0:n, 512:1024], in_=ps2[0:n, :]), [bps2], [bio])
                    if kb == 7:
                        Dma(R.dma_start(out=dst[b0:b0 + n, :], in_=io[0:n, :]), [bio], [], "D_" + bio.name)

                LOOK = 4
                for idx in range(min(LOOK, len(items))):
                    f_s(idx)
                for idx in range(len(items)):
                    f_t(idx)
                    if idx + LOOK < len(items):
                        f_s(idx + LOOK)

        setup()
        bC.const = True
        import os as _os
        _nsteps = int(_os.environ.get("MK_STEPS", "100000"))
        _k = 0
        for l in range(L):
            for t in tiles_of(l):
                tile_layer(l, t)
                if l == L - 1:
                    if t in (1, 2, 3, 4):
                        final_tiles([t])
                    if t == 5 or (t == 0 and L > 1 and RIDER):
                        final_tiles([5])
        P.wait_all("sp", [("d", n, 16 * c) for n, c in P.dma_count.items()])

        sems = {n: es.enter_context(nc.semaphore(n)) for n in P.sem_names()}
        with nc.Block() as block:
            @block.tensor
            def _(e):
                P.replay("pe", e, sems)

            @block.scalar
            def _(e):
                P.replay("act", e, sems)

            @block.vector
            def _(e):
                P.replay("dve", e, sems)

            @block.gpsimd
            def _(e):
                P.replay("pool", e, sems)

            @block.sync
            def _(e):
                P.replay("sp", e, sems)
    return nc


def _pack_weights(w_in, w_pa, w_pb, w_px, w_out, w_mk, w_mv, L):
    def k8(m):
        return m.reshape(8, 128, 512).transpose(1, 0, 2).reshape(128, 4096)

    def k4(m):
        return m.reshape(4, 128, 1024).transpose(1, 0, 2).reshape(128, 4096)

    out = np.empty((L, NCH, 128, 4096), np.float32)
    r = np.arange
    for l in range(L):
        wi = w_in[l]
        cols = {}
        cols[0] = r(0, 512)
        cols[1] = np.concatenate([r(512, 576), r(512, 576), r(576, 640), r(576, 640), r(640, 768), r(640, 768)])
        cols[2] = r(768, 1280)
        cols[3] = r(4352, 4864)
        cols[4] = r(4864, 5376)
        for j in range(4):
            cols[6 + j] = np.concatenate([r(1792 + 128 * j, 1920 + 128 * j), r(2304 + 128 * j, 2432 + 128 * j),
                                          r(1280 + 128 * j, 1408 + 128 * j), r(2816 + 128 * j, 2944 + 128 * j)])
        cols[10] = r(5376, 5888)
        cols[11] = r(5888, 6400)
        cols[13] = r(3328, 3840)
        cols[14] = r(3840, 4352)
        cols[15] = r(6400, 6912)
        cols[16] = r(6912, 7424)
        for c, idx in cols.items():
            out[l, c] = k8(wi[:, idx])
        out[l, 5] = k4(w_pa[l])
        out[l, 12] = k4(w_pb[l])
        out[l, 17] = k4(w_px[l])
        out[l, 18] = k8(w_out[l][:, 0:512])
        out[l, 19] = k8(w_out[l][:, 512:1024])
        out[l, 20] = k8(w_mk[l])
        out[l, 21] = k8(w_mv[l])
    return out


def _bias_tables():
    slopes = 2.0 ** (-8.0 * np.arange(1, 9) / 8.0)
    bp = np.zeros((128, 2, 8, 128), np.float32)
    qi = np.arange(128)[:, None]
    kk = np.arange(128)[None, :]
    for blk in range(2):
        kpos = kk - 128 if blk == 0 else kk
        dist = np.abs(qi - kpos).astype(np.float32)
        qc = qi // 64
        kc = np.floor_divide(kpos, 64)
        vis = (kc >= qc - 2) & (kc <= qc)
        for h in range(8):
            bp[:, blk, h, :] = np.where(vis, -slopes[h] * dist, NEG)
    bs = np.zeros((128, 2, 8, 128), np.float32)
    for blk in range(2):
        kpos = kk - 128 if blk == 0 else kk
        dist = np.abs(qi - kpos).astype(np.float32)
        for h in range(8):
            bs[:, blk, h, :] = -slopes[h] * dist
    return (bp.reshape(128, -1).astype(ml_dtypes.bfloat16), bs.reshape(128, -1).astype(ml_dtypes.bfloat16))


_PROG = {}


def make_in_maps(x_prompt, x_sample, cache_attn_k, cache_attn_v, cache_conv, cache_mem_k, cache_mem_v,
                 mem_prompt, norm_g, w_in, attn_sink, w_pa, conv_w, conv_b, w_pb, mem_norm_g,
                 w_mk, w_mv, w_px, w_out, final_g, L=DEPTH):
    f = lambda a: np.ascontiguousarray(np.asarray(a, dtype=np.float32))
    x_prompt, x_sample = f(x_prompt), f(x_sample)
    wpack = _pack_weights(f(w_in), f(w_pa), f(w_pb), f(w_px), f(w_out), f(w_mk), f(w_mv), L)
    bp, bs = _bias_tables()
    pvec = np.zeros((136, 128), np.float32)
    ng, mg, cwt, cbs = f(norm_g), f(mem_norm_g), f(conv_w), f(conv_b)
    for l in range(L):
        pvec[l * 32:l * 32 + 8] = ng[l].reshape(8, 128)
        pvec[l * 32 + 8:l * 32 + 16] = mg[l].reshape(8, 128)
        pvec[l * 32 + 16:l * 32 + 28] = cwt[l].reshape(3, 4, 128).reshape(12, 128)
        pvec[l * 32 + 28:l * 32 + 32] = cbs[l].reshape(4, 128)
    pvec[128:136] = f(final_g).reshape(8, 128)
    sinkb = np.zeros((128, 32), np.float32)
    sinkb[:, :L * 8] = np.broadcast_to(f(attn_sink)[:L].reshape(1, L * 8), (128, L * 8))
    ident = np.eye(128, dtype=np.float32)
    identb = ident.astype(ml_dtypes.bfloat16)
    ck, cv, cc_, cmk, cmv, memp = f(cache_attn_k), f(cache_attn_v), f(cache_conv), f(cache_mem_k), f(cache_mem_v), f(mem_prompt)

    in_maps = []
    for i in range(NCORE):
        b, j = divmod(i, 4)
        xin = np.zeros((NTOK, D), np.float32)
        a = j * OWN
        if j > 0:
            xin[0:HALO] = x_prompt[b, a - HALO:a]
        xin[HALO:NP] = x_prompt[b, a:a + OWN]
        xin[NP:] = x_sample[i]
        km = np.zeros((128, 22), np.float32)
        km[:, 0] = NEG
        if j == 0:
            km[:, 1:5] = NEG
        in_maps.append(dict(
            xin=xin, wpack=wpack, pvec=pvec, sinkb=sinkb, kmask=km, ident=ident, identb=identb, biasp=bp,
            memp=memp[b],
            ck=np.ascontiguousarray(ck[:L, i].reshape(L, 128, 128)), cv=np.ascontiguousarray(cv[:L, i].reshape(L, 128, 128)),
            cconv=np.ascontiguousarray(cc_[:L, i]),
            cmk=np.ascontiguousarray(cmk[:L, i].reshape(L, 256, 512)), cmv=np.ascontiguousarray(cmv[:L, i].reshape(L, 256, 512)),
        ))
    return in_maps


def kernel(x_prompt, x_sample, cache_attn_k, cache_attn_v, cache_conv, cache_mem_k, cache_mem_v,
           mem_prompt, norm_g, w_in, attn_sink, w_pa, conv_w, conv_b, w_pb, mem_norm_g,
           w_mk, w_mv, w_px, w_out, final_g, _n_layers=DEPTH):
    L = _n_layers
    in_maps = make_in_maps(x_prompt, x_sample, cache_attn_k, cache_attn_v, cache_conv, cache_mem_k, cache_mem_v,
                           mem_prompt, norm_g, w_in, attn_sink, w_pa, conv_w, conv_b, w_pb, mem_norm_g,
                           w_mk, w_mv, w_px, w_out, final_g, L=L)
    if L not in _PROG:
        _PROG[L] = build_program(L)
    res = run_bass_kernel_spmd(_PROG[L], in_maps, core_ids=list(range(NCORE)))
    R = res.results
    y_prompt = np.stack([np.concatenate([R[b * 4 + j]["y_own"] for j in range(4)], axis=0) for b in range(2)])
    y_sample = np.stack([R[i]["y_smp"] for i in range(NCORE)])
    pk = np.stack([R[3]["ko"], R[7]["ko"]], axis=1).reshape(L, 2, 128, 2, 64)
    pv = np.stack([R[3]["vo"], R[7]["vo"]], axis=1).reshape(L, 2, 128, 2, 64)
    pc = np.stack([R[3]["co"], R[7]["co"]], axis=1).reshape(L, 2, 2, 512)
    pmk = np.stack([R[0]["mko"], R[4]["mko"]], axis=1).reshape(L, 2, 256, 4, 128)
    pmv = np.stack([R[0]["mvo"], R[4]["mvo"]], axis=1).reshape(L, 2, 256, 4, 128)
    sk = np.stack([R[i]["kso"] for i in range(NCORE)], axis=1).reshape(L, NCORE, TS, 2, 64)
    sv = np.stack([R[i]["vso"] for i in range(NCORE)], axis=1).reshape(L, NCORE, TS, 2, 64)
    sc = np.stack([R[i]["cso"] for i in range(NCORE)], axis=1).reshape(L, NCORE, 2, 512)
    outs = (y_prompt, y_sample, pk, pv, pc, pmk, pmv, sk, sv, sc)
    return tuple(np.ascontiguousarray(o.astype(np.float32)) for o in outs)
```

```python
import bisect
from contextlib import ExitStack

import numpy as np
import ml_dtypes

import concourse.bass as bass
import concourse.mybir as mybir
from concourse.bass_utils import run_bass_kernel_spmd

F32 = mybir.dt.float32
BF16 = mybir.dt.bfloat16
AF = mybir.ActivationFunctionType
ALU = mybir.AluOpType
AX = mybir.AxisListType

DEPTH = 4
D = 1024
NCORE = 8
OWN = 2048
HALO = 512
NP = HALO + OWN
TS = 32
NTOK = NP + TS
NCH = 22
WR_SLOTS = 3
EPS = 1e-6
NEG = -30000.0
RIDER = True


class Buf:
    __slots__ = ("name", "w", "r", "const", "excl")

    def __init__(self, name, excl=False):
        self.name = name
        self.w = None
        self.r = []
        self.const = False
        self.excl = excl


class _Op:
    __slots__ = ("fn", "waits", "inc", "dma_inc")

    def __init__(self, fn):
        self.fn = fn
        self.waits = []
        self.inc = False
        self.dma_inc = None


class _Eng:
    def __init__(self, name):
        self.name = name
        self.ops = []
        self.inc_idx = []
        self.count = 0
        self.waited = {}


class Plan:
    ENGS = ("pe", "act", "dve", "pool", "sp")

    def __init__(self):
        self.e = {n: _Eng(n) for n in self.ENGS}
        self.dma_count = {}

    def _resolve(self, tok):
        if tok[0] == "d":
            return tok[1], tok[2]
        eng = self.e[tok[1]]
        n = tok[2]
        k = bisect.bisect_left(eng.inc_idx, n)
        if k < len(eng.inc_idx):
            return "E_" + eng.name, k + 1
        eng.ops[n].inc = True
        eng.inc_idx.append(n)
        eng.count += 1
        return "E_" + eng.name, eng.count

    def emit(self, eng, fn, reads=(), writes=(), dma=None, extra=()):
        E = self.e[eng]
        deps = [t for t in extra if t is not None]
        for b in reads:
            if b.w is not None:
                deps.append(b.w)
            if b.excl:
                deps.extend(d for d in b.r if not (d[0] == "c" and d[1] == eng))
        for b in writes:
            if b.w is not None:
                deps.append(b.w)
            deps.extend(b.r)
        op = _Op(fn)
        for d in deps:
            if dma is None and d[0] == "c" and d[1] == eng and eng == "pe":
                continue
            sem, val = self._resolve(d)
            if E.waited.get(sem, 0) < val:
                E.waited[sem] = val
                op.waits.append((sem, val))
        idx = len(E.ops)
        E.ops.append(op)
        if dma is None:
            tok = ("c", eng, idx)
        else:
            c = self.dma_count.get(dma, 0) + 1
            self.dma_count[dma] = c
            op.dma_inc = dma
            tok = ("d", dma, 16 * c)
        for b in reads:
            if not b.const:
                b.r.append(tok)
        for b in writes:
            b.w = tok
            b.r = []
        return tok

    def wait_all(self, eng, toks):
        E = self.e[eng]
        op = _Op(None)
        for d in toks:
            sem, val = self._resolve(d)
            if E.waited.get(sem, 0) < val:
                E.waited[sem] = val
                op.waits.append((sem, val))
        E.ops.append(op)

    def sem_names(self):
        return ["E_" + n for n in self.ENGS] + sorted(self.dma_count.keys())

    def replay(self, eng, handle, sems):
        for op in self.e[eng].ops:
            for sem, val in op.waits:
                handle.wait_ge(sems[sem], val)
            if op.fn is None:
                continue
            ins = op.fn(handle)
            if op.inc:
                ins.then_inc(sems["E_" + eng], 1)
            if op.dma_inc is not None:
                ins.then_inc(sems[op.dma_inc], 16)


class _Rec:
    def __getattr__(self, name):
        def mk(*a, **k):
            return lambda handle: getattr(handle, name)(*a, **k)
        return mk


R = _Rec()


def build_program(n_layers=DEPTH):
    nc = bass.Bass("TRN2", target_bir_lowering=False)
    L = n_layers

    def din(name, shape, dt=F32):
        return nc.dram_tensor(name, list(shape), dt, kind="ExternalInput").ap()

    def dout(name, shape, dt=F32):
        return nc.dram_tensor(name, list(shape), dt, kind="ExternalOutput").ap()

    xin = din("xin", [NTOK, D])
    wpack = din("wpack", [L, NCH, 128, 4096])
    pvec = din("pvec", [136, 128])
    sinkb = din("sinkb", [128, 32])
    kmask = din("kmask", [128, 22])
    ident_d = din("ident", [128, 128])
    identb_d = din("identb", [128, 128], BF16)
    biasp_d = din("biasp", [128, 2 * 8 * 128], BF16)
    memp = din("memp", [256, D])
    ck_d = din("ck", [L, 128, 128])
    cv_d = din("cv", [L, 128, 128])
    cconv_d = din("cconv", [L, 2, 512])
    cmk_d = din("cmk", [L, 256, 512])
    cmv_d = din("cmv", [L, 256, 512])

    y_own = dout("y_own", [OWN, D])
    y_smp = dout("y_smp", [TS, D])
    ko = dout("ko", [L, 128, 128])
    vo = dout("vo", [L, 128, 128])
    co = dout("co", [L, 2, 512])
    mko = dout("mko", [L, 256, 512])
    mvo = dout("mvo", [L, 256, 512])
    kso = dout("kso", [L, TS, 128])
    vso = dout("vso", [L, TS, 128])
    cso = dout("cso", [L, 2, 512])

    wscr = nc.dram_tensor("wscr", [L, NCH, 128, 4096], BF16, kind="Internal").ap()
    memt_scr = nc.dram_tensor("memt_scr", [128, 8 * 256], F32, kind="Internal").ap()

    P = Plan()
    es = ExitStack()
    with es:
        def sb(name, shape, dt):
            return es.enter_context(nc.sbuf_tensor(name, list(shape), dt))

        def psum(name, shape, dt):
            return es.enter_context(nc.psum_tensor(name, list(shape), dt))

        X = sb("X", [128, 8, NTOK], F32)
        XNa = sb("XNa", [128, 8, 512], BF16)
        XNb = sb("XNb", [128, 8, 512], BF16)
        XN2 = [XNa, XNb]
        MACC = sb("MACC", [128, 8, 512], BF16)
        FT = sb("FT", [128, 8, 512], F32)
        QT = sb("QT", [128, 4, 512], BF16)
        SG = sb("SG", [128, 4, 512], BF16)
        AT = sb("AT", [128, 4, 512], BF16)
        PB = sb("PB", [128, 4, 512], BF16)
        KT = sb("KT", [128, 2, 640], BF16)
        KTs = sb("KTs", [128, 2, 160], BF16)
        VA = sb("VA", [128, 5, 2, 66], BF16)
        VAs = sb("VAs", [128, 2, 2, 66], BF16)
        ATOK = sb("ATOK", [128, 3, 4, 64], F32)
        XTOK = sb("XTOK", [128, 3, 2, 128], F32)
        DEN = sb("DEN", [128, 3, 4], F32)
        BIASP = sb("BIASP", [128, 2, 8, 128], BF16)
        UJ = sb("UJ", [128, 2, 514], F32)
        ULB = sb("ULB", [128, 4, 2], F32)
        ULBs = sb("ULBs", [128, 4, 2], F32)
        MKT = sb("MKT", [128, 4, 256], BF16)
        MVA = sb("MVA", [128, 2, 4, 130], BF16)
        MKTs = sb("MKTs", [128, 4, 256], BF16)
        MVAs = sb("MVAs", [128, 2, 4, 130], BF16)
        WR = sb("WR", [128, WR_SLOTS, 4096], BF16)
        IO = sb("IO", [128, 2, 1024], F32)
        KF = sb("KF", [128, 160], F32)
        IDENT = sb("IDENT", [128, 128], F32)
        IDB = sb("IDB", [128, 128], BF16)
        ONES = sb("ONES", [128, 128], F32)
        ONESB = sb("ONESB", [128, 128], BF16)
        PV = sb("PV", [128, 136], F32)
        ESINK = sb("ESINK", [128, 32], F32)
        KM = sb("KM", [128, 22], F32)
        EPSC = sb("EPSC", [128, 1], F32)
        MN = PB[:].rearrange("p a b -> p (a b)").rearrange("p (k m) -> p k m", k=8)

        PR = [psum(f"PR{i}", [128, 512], F32) for i in range(3)]
        SP = [psum(f"SP{i}", [128, 512], F32) for i in range(4)]
        OP = psum("OP", [128, 512], F32)

        bX = [Buf(f"X{t}") for t in range(6)]
        bXN2 = [Buf("XNa"), Buf("XNb")]
        bMACC = [Buf(f"MACC{i}") for i in range(8)]
        bFT = [Buf(f"FT{i}") for i in range(8)]
        bQT = [Buf(f"QT{i}") for i in range(4)]
        bSG = [Buf(f"SG{i}") for i in range(4)]
        bAT = [Buf(f"AT{i}") for i in range(4)]
        bPB = [Buf(f"PB{i}") for i in range(4)]
        bKT, bKTs, bVA, bVAs = Buf("KT"), Buf("KTs"), Buf("VA"), Buf("VAs")
        bATOK2 = [Buf("ATOK0"), Buf("ATOK1"), Buf("ATOK2")]
        bXTOK2 = [Buf("XTOK0"), Buf("XTOK1"), Buf("XTOK2")]
        bDEN2 = [Buf("DEN0"), Buf("DEN1"), Buf("DEN2")]
        bDEN = bDEN2[0]
        bUJ = [Buf("UJ0"), Buf("UJ1")]
        bULB, bULBs = Buf("ULB"), Buf("ULBs")
        bMKT, bMVA, bMKTs, bMVAs = Buf("MKT"), Buf("MVA"), Buf("MKTs"), Buf("MVAs")
        bWR = [Buf(f"WR{i}") for i in range(WR_SLOTS)]
        bIO = [Buf("IO0"), Buf("IO1")]
        bKF = Buf("KF")
        bC = Buf("CONST")
        bPR = [Buf(f"PR{i}", excl=True) for i in range(3)]
        bSP = [Buf(f"SP{i}", excl=True) for i in range(4)]
        bOP = Buf("OP", excl=True)
        bWC = {}
        bWCS = [Buf(f"WCS{c}") for c in range(NCH)]
        bMEMT = Buf("MEMT")

        def Aop(fn, r=(), w=()):
            return P.emit("act", fn, r, w)

        def Vop(fn, r=(), w=()):
            return P.emit("dve", fn, r, w)

        def Gop(fn, r=(), w=()):
            return P.emit("pool", fn, r, w)

        def Top(fn, r=(), w=()):
            return P.emit("pe", fn, r, w)

        def Dma(fn, r, w, sem, eng="sp"):
            return P.emit(eng, fn, r, w, dma=sem)

        pr_i = [0]
        pr_wide = [False]

        def pr_next():
            n = 8 if pr_wide[0] else 4
            k = pr_i[0] % n
            pr_i[0] += 1
            if k < 3:
                return PR[k], bPR[k]
            if k == 3:
                return OP, bOP
            return SP[k - 4], bSP[k - 4]

        ft_i = [0]

        def ft_next():
            k = ft_i[0] % 8
            ft_i[0] += 1
            return FT[:, k, :], bFT[k]

        io_i = [0]

        def io_next():
            k = io_i[0] % 2
            io_i[0] += 1
            return IO[:, k, :], bIO[k]

        def tiles_of(l):
            return [0, 1, 2, 3, 4, 5] if (l == 0 or not RIDER) else [0, 1, 2, 3, 4]

        wseq = []
        tile_seq = []
        layer_start = []
        for l in range(L):
            layer_start.append(len(wseq))
            for t in tiles_of(l):
                tile_seq.append((l, t))
                for c in range(20):
                    wseq.append((l, c))
                    if c == 12 and l == 0 and t == 0:
                        wseq.append((0, 20))
                        wseq.append((0, 21))
                    if c == 14 and t == tiles_of(l)[-1] and l + 1 < L:
                        wseq.append((l + 1, 20))
                        wseq.append((l + 1, 21))
        layer_start.append(len(wseq))
        conv_done = set()

        def emit_convert(l, c, after=()):
            if (l, c) in conv_done or l >= L:
                return
            conv_done.add((l, c))
            b = Buf(f"WC{l}_{c}")
            bWC[(l, c)] = b
            P.emit("pool", R.dma_start(out=wscr[l, c], in_=wpack[l, c]), [], [b, bWCS[c]], dma=f"D_wc{c}", extra=list(after))

        def w_load(i):
            if i >= len(wseq):
                return
            l, c = wseq[i]
            emit_convert(l, c)
            s = i % WR_SLOTS
            Dma(R.dma_start(out=WR[:, s, :], in_=wscr[l, c]), [bWC[(l, c)]], [bWR[s]], f"D_wr{s}")

        w_pos = [0]

        def w_acquire(l, c):
            i = w_pos[0]
            assert wseq[i] == (l, c), (wseq[i], l, c)
            w_pos[0] += 1
            s = i % WR_SLOTS
            return i, WR[:, s, :], bWR[s]

        nxt_order = [20, 21] + list(range(20))
        per_layer = 2 + 6 * 20

        def w_release(i):
            rd = bWR[i % WR_SLOTS].r
            tok = rd[-1] if rd else None
            w_load(i + WR_SLOTS)
            j = i + 5
            if j < 24:
                emit_convert(*wseq[j], after=[tok])
            l = max(ll for ll in range(L) if layer_start[ll] <= i) if i >= layer_start[0] else 0
            r = i - layer_start[l]
            step = max(1, (layer_start[l + 1] - layer_start[l] - 40) // NCH)
            if r >= 0 and r % step == 0 and r // step < NCH and l + 1 < L:
                emit_convert(l + 1, nxt_order[r // step], after=[tok])

        def setup():
            Dma(R.dma_start(out=IDENT[:], in_=ident_d), [], [bC], "D_c")
            Dma(R.dma_start(out=IDB[:], in_=identb_d), [], [bC], "D_c")
            Dma(R.dma_start(out=BIASP[:].rearrange("p a b c -> p (a b c)"), in_=biasp_d), [], [bC], "D_c")
            Dma(R.dma_start(out=ESINK[:], in_=sinkb), [], [bC], "D_c")
            Dma(R.dma_start(out=KM[:], in_=kmask), [], [bC], "D_c")
            Vop(R.memset(ONES[:], 1.0), [], [bC])
            Vop(R.memset(ONESB[:], 1.0), [], [bC])
            Vop(R.memset(EPSC[:], EPS), [], [bC])
            Vop(R.memset(VA[:].rearrange("p a b c -> p (a b c)"), 1.0), [], [bVA])
            Vop(R.memset(VAs[:].rearrange("p a b c -> p (a b c)"), 1.0), [], [bVAs])
            Vop(R.memset(MVA[:].rearrange("p a b c -> p (a b c)"), 1.0), [], [bMVA])
            Vop(R.memset(MVAs[:].rearrange("p a b c -> p (a b c)"), 1.0), [], [bMVAs])
            Aop(R.activation(out=ESINK[:], in_=ESINK[:], func=AF.Exp), [bC], [bC])
            io, bio = io_next()
            Dma(R.dma_start(out=io[:, 0:128], in_=pvec[0:128, :]), [], [bio], "D_" + bio.name)
            Dma(R.dma_start(out=io[0:8, 128:256], in_=pvec[128:136, :]), [], [bio], "D_" + bio.name)
            ps, bps = pr_next()
            Top(R.transpose(ps[:, 0:128], io[:, 0:128], IDENT[:]), [bio, bC], [bps])
            Top(R.transpose(ps[:, 128:136], io[0:8, 128:256], IDENT[0:8, 0:8]), [bio, bC], [bps])
            Vop(R.tensor_copy(out=PV[:], in_=ps[:, 0:136]), [bps], [bC])
            for (l, c) in wseq[:5]:
                emit_convert(l, c)
            for i in range(WR_SLOTS):
                w_load(i)
            for rb in range(21):
                n = 128 if rb < 20 else TS
                r0 = rb * 128
                io, bio = io_next()
                Dma(R.dma_start(out=io[0:n, :], in_=xin[r0:r0 + n, :]), [], [bio], "D_" + bio.name)
                bx = bX[min(rb // 4, 5)]
                for hf in range(2):
                    ps, bps = pr_next()
                    for k4 in range(4):
                        kb = hf * 4 + k4
                        Top(R.transpose(ps[:, k4 * 128:k4 * 128 + n], io[0:n, kb * 128:(kb + 1) * 128], IDENT[0:n, 0:n]),
                            [bio, bC], [bps])
                    src = ps[:].rearrange("p (k c) -> p k c", k=4)[:, :, 0:n]
                    dst = X[:, hf * 4:hf * 4 + 4, r0:r0 + n]
                    if hf == 0:
                        Vop(R.tensor_copy(out=dst, in_=src), [bps], [bx])
                    else:
                        Aop(R.activation(out=dst, in_=src, func=AF.Copy), [bps], [bx])
        def setup_mem():
            for mb in range(2):
                io, bio = io_next()
                io2, bio2 = io_next()
                Dma(R.dma_start(out=io, in_=memp[mb * 128:(mb + 1) * 128, :]), [], [bio], "D_" + bio.name)
                Aop(R.activation(out=io2, in_=io, func=AF.Square), [bio], [bio2])
                Vop(R.reduce_sum(out=DEN[:, 0, 0:1], in_=io2, axis=AX.X), [bio2], [bDEN])
                Aop(R.activation(out=DEN[:, 0, 0:1], in_=DEN[:, 0, 0:1], func=AF.Sqrt, scale=1.0 / D, bias=EPSC[:, 0:1]), [bDEN, bC], [bDEN])
                Vop(R.reciprocal(out=DEN[:, 0, 1:2], in_=DEN[:, 0, 0:1]), [bDEN], [bDEN])
                Vop(R.tensor_scalar(out=io, in0=io, scalar1=DEN[:, 0, 1:2], scalar2=None, op0=ALU.mult), [bio, bDEN], [bio])
                for hf in range(2):
                    ps, bps = pr_next()
                    for k4 in range(4):
                        kb = hf * 4 + k4
                        Top(R.transpose(ps[:, k4 * 128:(k4 + 1) * 128], io[:, kb * 128:(kb + 1) * 128], IDENT[:]),
                            [bio, bC], [bps])
                    Vop(R.tensor_copy(out=io2[:, hf * 512:(hf + 1) * 512], in_=ps[:]), [bps], [bio2])
                dst = memt_scr.rearrange("p (k m) -> p k m", k=8)[:, :, mb * 128:(mb + 1) * 128]
                Dma(R.dma_start(out=dst, in_=io2.rearrange("p (k m) -> p k m", k=8)), [bio2], [bMEMT], "D_" + bio2.name)

        def proj(wt, bw, lhs_of_kb, rhs, brhs, T, nkb=8):
            ps, bps = pr_next()
            for kb in range(nkb):
                Top(R.matmul(ps[:, 0:T], lhsT=lhs_of_kb(kb), rhs=rhs(kb), start=(kb == 0), stop=(kb == nkb - 1)),
                    [bw] + list(brhs), [bps])
            return ps, bps

        def w8(wt, cb):
            v = wt.rearrange("p (k c) -> p k c", k=8)
            return lambda kb: v[:, kb, cb * 128:(cb + 1) * 128]

        def w4(wt, ob):
            v = wt.rearrange("p (k c) -> p k c", k=4)
            return lambda kb: v[:, kb, ob * 128:(ob + 1) * 128]

        def tanh_gate(ps, bps, T, out_ap, bout):
            tg, btg = ft_next()
            Aop(R.activation(out=tg[:, 0:T], in_=ps[:, 0:T], func=AF.Tanh, scale=0.5), [bps], [btg])
            Vop(R.scalar_tensor_tensor(out=out_ap, in0=tg[:, 0:T], scalar=1.0, in1=ps[:, 0:T], op0=ALU.add, op1=ALU.mult), [btg, bps], [bout])

        def merge_branch(l, cm0, cm1, cp, T, first, last, XN, bXN):
            i0, wm0, bwm0 = w_acquire(l, cm0)
            i1, wm1, bwm1 = w_acquire(l, cm1)
            ip, wp, bwp = w_acquire(l, cp)

            def gate(ob):
                wm, bwm = (wm0, bwm0) if ob < 4 else (wm1, bwm1)
                ps, bps = proj(wm, bwm, w8(wm, ob % 4), lambda kb: XN[:, kb, 0:T], [bXN], T)
                tm, btm = ft_next()
                Aop(R.activation(out=tm[:, 0:T], in_=ps[:, 0:T], func=AF.Tanh, scale=0.5), [bps], [btm])
                if ob == 3:
                    w_release(i0)
                return tm, btm

            pend = gate(0)
            for ob in range(8):
                tm, btm = pend
                if ob + 1 < 8:
                    pend = gate(ob + 1)
                ps2, bps2 = proj(wp, bwp, w4(wp, ob), lambda kb: AT[:, kb, 0:T], bAT, T, nkb=4)
                if first:
                    Vop(R.scalar_tensor_tensor(out=MACC[:, ob, 0:T], in0=tm[:, 0:T], scalar=1.0, in1=ps2[:, 0:T],
                                               op0=ALU.add, op1=ALU.mult), [btm, bps2], [bMACC[ob]])
                else:
                    Vop(R.scalar_tensor_tensor(out=tm[:, 0:T], in0=tm[:, 0:T], scalar=1.0, in1=ps2[:, 0:T],
                                               op0=ALU.add, op1=ALU.mult), [btm, bps2], [btm])
                    Vop(R.tensor_tensor(out=MACC[:, ob, 0:T], in0=MACC[:, ob, 0:T], in1=tm[:, 0:T], op=ALU.add),
                        [btm, bMACC[ob]], [bMACC[ob]])
            w_release(i1)
            w_release(ip)

        def out_rows(src_ps, bps, n, ncols, dst):
            io, bio = io_next()
            Aop(R.activation(out=io[0:n, 0:ncols], in_=src_ps, func=AF.Copy), [bps], [bio])
            Dma(R.dma_start(out=dst, in_=io[0:n, 0:ncols]), [bio], [], "D_" + bio.name)

        import os as _os2
        _sub = int(_os2.environ.get("MK_SUB", "99"))

        def layer_setup(l):
            mgcol = l * 32 + 8
            for hf in range(2):
                io, bio = io_next()
                Dma(R.dma_start(out=io, in_=memt_scr[:, hf * 1024:(hf + 1) * 1024]), [bMEMT], [bio], "D_" + bio.name)
                for k4 in range(4):
                    kb = hf * 4 + k4
                    Vop(R.tensor_scalar(out=MN[:, kb, :], in0=io[:, k4 * 256:(k4 + 1) * 256], scalar1=PV[:, mgcol + kb:mgcol + kb + 1],
                                                                   scalar2=None, op0=ALU.mult), [bio, bC], bPB)
            if _sub < 1:
                return
            ik, wk, bwk = w_acquire(l, 20)
            iv, wv, bwv = w_acquire(l, 21)
            wk8 = wk.rearrange("p (k c) -> p k c", k=8)
            wv8 = wv.rearrange("p (k c) -> p k c", k=8)
            for mb in range(2):
                ps, bps = proj(wk, bwk, lambda kb, mb=mb: MN[:, kb, mb * 128:(mb + 1) * 128], lambda kb: wk8[:, kb, :], bPB, 512)
                out_rows(ps[:, 0:512], bps, 128, 512, mko[l, mb * 128:(mb + 1) * 128, :])
            if _sub < 2:
                return
            for h in range(4):
                ps, bps = proj(wk, bwk, lambda kb, h=h: wk8[:, kb, h * 128:(h + 1) * 128], lambda kb: MN[:, kb, :], bPB, 256)
                Vop(R.tensor_copy(out=MKT[:, h, :], in_=ps[:, 0:256]), [bps], [bMKT])
            if _sub < 3:
                return
            for mb in range(2):
                ps, bps = proj(wv, bwv, lambda kb, mb=mb: MN[:, kb, mb * 128:(mb + 1) * 128], lambda kb: wv8[:, kb, :], bPB, 512)
                if _os2.environ.get("MK_X1", "0") != "1":
                    Vop(R.tensor_copy(out=MVA[:, mb, :, 0:128], in_=ps[:].rearrange("p (h d) -> p h d", h=4)), [bps], [bMVA])
                out_rows(ps[:, 0:512], bps, 128, 512, mvo[l, mb * 128:(mb + 1) * 128, :])
            if _os2.environ.get("MK_X2", "0") != "1":
                w_release(ik)
                w_release(iv)
            if _sub < 4:
                return
            io, bio = io_next()
            ckv = ck_d[l].rearrange("t (k d) -> t k d", k=2)
            for dup in range(2):
                Dma(R.dma_start(out=io.rearrange("p (k u d) -> p k u d", k=2, u=8)[:, :, dup, :], in_=ckv), [], [bio], "D_" + bio.name)
            for kv in range(2):
                ps, bps = pr_next()
                Top(R.transpose(ps[:, 0:128], io[:, kv * 512:kv * 512 + 128], IDENT[:]), [bio, bC], [bps])
                Vop(R.tensor_copy(out=KTs[:, kv, 0:128], in_=ps[:, 0:128]), [bps], [bKTs])
            if _sub < 5:
                return
            io, bio = io_next()
            Dma(R.dma_start(out=io[:, 0:128], in_=cv_d[l]), [], [bio], "D_" + bio.name)
            Dma(R.dma_start(out=io[0:2, 128:640], in_=cconv_d[l]), [], [bio], "D_" + bio.name)
            Vop(R.tensor_copy(out=VAs[:, 0, :, 0:64], in_=io[:, 0:128].rearrange("p (k d) -> p k d", k=2)), [bio], [bVAs])
            ps, bps = pr_next()
            for j in range(4):
                Top(R.transpose(ps[:, j * 2:j * 2 + 2], io[0:2, 128 + j * 128:256 + j * 128], IDENT[0:2, 0:2]), [bio, bC], [bps])
            Aop(R.activation(out=ULBs[:].rearrange("p a b -> p (a b)"), in_=ps[:, 0:8], func=AF.Copy), [bps], [bULBs])
            if _sub < 6:
                return
            for mb in range(2):
                io, bio = io_next()
                Dma(R.dma_start(out=io[:, 0:512], in_=cmk_d[l, mb * 128:(mb + 1) * 128, :]), [], [bio], "D_" + bio.name)
                Dma(R.dma_start(out=io[:, 512:1024], in_=cmv_d[l, mb * 128:(mb + 1) * 128, :]), [], [bio], "D_" + bio.name)
                ps, bps = pr_next()
                for h in range(4):
                    Top(R.transpose(ps[:, h * 128:(h + 1) * 128], io[:, h * 128:(h + 1) * 128], IDENT[:]), [bio, bC], [bps])
                Vop(R.tensor_copy(out=MKTs[:, :, mb * 128:(mb + 1) * 128], in_=ps[:].rearrange("p (h m) -> p h m", h=4)), [bps], [bMKTs])
                Vop(R.tensor_copy(out=MVAs[:, mb, :, 0:128], in_=io[:, 512:1024].rearrange("p (h d) -> p h d", h=4)), [bio], [bMVAs])

        def tparams(l, t):
            if t == 5:
                return 0, 0, TS
            if t == 0:
                return 512 - 128 * l, 128 * l, (TS if (RIDER and l >= 1) else 0)
            return 512, 512 * t, 0

        def tsegs(l, t):
            T, c0, E = tparams(l, t)
            segs = []
            if T:
                segs.append((0, T, c0, bX[t]))
            if E:
                segs.append((T, E, NP, bX[5]))
            return T, c0, E, segs

        tile_ctr = [0]
        norm_pending = {}
        norm_now = set()

        sq_pending = {}

        def norm_part1a(l, t, k):
            T, c0, E, segs = tsegs(l, t)
            sqs = []
            for pair in range(4):
                sq, bsq = ft_next()
                sqb = sq.bitcast(BF16)
                for half in range(2):
                    kb = 2 * pair + half
                    for (lo, n, xc, bx) in segs:
                        Aop(R.activation(out=sqb[:, half * 512 + lo:half * 512 + lo + n], in_=X[:, kb, xc:xc + n], func=AF.Square), [bx], [bsq])
                    sqs.append((sqb[:, half * 512:(half + 1) * 512], bsq))
            sq_pending[k] = sqs

        def norm_part1(l, t, k):
            T, c0, E, segs = tsegs(l, t)
            W = T + E
            if k not in sq_pending:
                norm_part1a(l, t, k)
            sqs = sq_pending.pop(k)
            ps, bps = pr_next()
            for kb in range(8):
                sq, bsq = sqs[kb]
                Top(R.matmul(ps[:, 0:W], lhsT=ONESB[:], rhs=sq[:, 0:W], start=(kb == 0), stop=(kb == 7)), [bsq, bC], [bps])
            ms, bms = ft_next()
            Aop(R.activation(out=ms[:, 0:W], in_=ps[:, 0:W], func=AF.Sqrt, scale=1.0 / D, bias=EPSC[:, 0:1]), [bps, bC], [bms])
            Vop(R.reciprocal(out=ms[:, 0:W], in_=ms[:, 0:W]), [bms], [bms])
            norm_pending[k] = (ms, bms)

        def norm_part2(l, t, k):
            T, c0, E, segs = tsegs(l, t)
            XN, bXN = XN2[k % 2], bXN2[k % 2]
            ms, bms = norm_pending.pop(k)
            gcol = l * 32
            for kb in range(8):
                for (lo, n, xc, bx) in segs:
                    Vop(R.scalar_tensor_tensor(out=XN[:, kb, lo:lo + n], in0=X[:, kb, xc:xc + n], scalar=PV[:, gcol + kb:gcol + kb + 1],
                                               in1=ms[:, lo:lo + n], op0=ALU.mult, op1=ALU.mult), [bx, bms, bC], [bXN])

        def tile_layer(l, t):
            T, c0, E, segs = tsegs(l, t)
            W = T + E
            has_p, has_s = T > 0, E > 0
            k = tile_ctr[0]
            tile_ctr[0] += 1
            XN, bXN = XN2[k % 2], bXN2[k % 2]
            gcol = l * 32
            bias = BIASP

            if t == 0:
                Vop(R.memset(ULB[:].rearrange("p a b -> p (a b)"), 0.0), [], [bULB])
                Vop(R.memset(KT[:, :, 0:128], 0.0), [], [bKT])
            if k == 0 or k in norm_now:
                norm_part1(l, t, k)
                norm_part2(l, t, k)
            xr = lambda kb: XN[:, kb, 0:W]

            i0, w0, bw0 = w_acquire(l, 0)
            for j in range(4):
                ps, bps = proj(w0, bw0, w8(w0, j), xr, [bXN], W)
                Aop(R.activation(out=QT[:, j, 0:W], in_=ps[:, 0:W], func=AF.Identity, scale=0.125), [bps], [bQT[j]])
            w_release(i0)
            i1, w1, bw1 = w_acquire(l, 1)
            for kv in range(2):
                ps, bps = proj(w1, bw1, w8(w1, kv), xr, [bXN], W)
                lo = kv * 64
                if has_p:
                    Vop(R.tensor_copy(out=KT[:, kv, 128:128 + T], in_=ps[:, 0:T]), [bps], [bKT])
                if has_s:
                    Vop(R.tensor_copy(out=KTs[:, kv, 128:128 + E], in_=ps[:, T:W]), [bps], [bKTs])
                    Aop(R.activation(out=KF[lo:lo + 64, 128:160], in_=ps[lo:lo + 64, T:W], func=AF.Copy), [bps], [bKF])
                if t == 4:
                    Aop(R.activation(out=KF[lo:lo + 64, 0:128], in_=ps[lo:lo + 64, 384:512], func=AF.Copy), [bps], [bKF])
            ps, bps = proj(w1, bw1, w8(w1, 2), xr, [bXN], W)
            vt, bvt = ft_next()
            Aop(R.activation(out=vt[:, 0:W], in_=ps[:, 0:W], func=AF.Copy), [bps], [bvt])
            w_release(i1)
            if has_s:
                ps, bps = pr_next()
                Top(R.transpose(ps[0:TS, 0:128], KF[:, 128:160], IDENT[:]), [bKF, bC], [bps])
                out_rows(ps[0:TS, 0:128], bps, TS, 128, kso[l])
            if t == 4:
                ps, bps = pr_next()
                Top(R.transpose(ps[:, 0:128], KF[:, 0:128], IDENT[:]), [bKF, bC], [bps])
                out_rows(ps[:, 0:128], bps, 128, 128, ko[l])
            nvb = T // 128
            for i in range(nvb):
                ps, bps = pr_next()
                Top(R.transpose(ps[:, 0:128], vt[:, i * 128:(i + 1) * 128], IDENT[:]), [bvt, bC], [bps])
                Vop(R.tensor_copy(out=VA[:, 1 + i, :, 0:64], in_=ps[:, 0:128].rearrange("p (k d) -> p k d", k=2)), [bps], [bVA])
                if t == 4 and i == nvb - 1:
                    out_rows(ps[:, 0:128], bps, 128, 128, vo[l])
            if has_s:
                ps, bps = pr_next()
                Top(R.transpose(ps[0:TS, 0:128], vt[:, T:W], IDENT[:]), [bvt, bC], [bps])
                Vop(R.tensor_copy(out=VAs[0:TS, 1, :, 0:64], in_=ps[0:TS, 0:128].rearrange("p (k d) -> p k d", k=2)), [bps], [bVAs])
                out_rows(ps[0:TS, 0:128], bps, TS, 128, vso[l])
            nxt = tile_seq[k + 1] if k + 1 < len(tile_seq) else None
            if nxt is not None and has_s and tparams(nxt[0], nxt[1])[2] > 0:
                norm_now.add(k + 1)
                nxt = None
            if nxt is not None:
                norm_part1a(nxt[0], nxt[1], k + 1)
            i2, w2, bw2 = w_acquire(l, 2)
            for j in range(4):
                ps, bps = proj(w2, bw2, w8(w2, j), xr, [bXN], W)
                tanh_gate(ps, bps, W, SG[:, j, 0:W], bSG[j])
            w_release(i2)

            if nxt is not None:
                norm_part1(nxt[0], nxt[1], k + 1)

            units = []
            for pr in range(T // 128):
                for g in range(2):
                    units.append(dict(c=128 * pr, nq=128, kt=KT, bkt=bKT, va=VA, bva=bVA, vbA=pr, kA=128 * pr, smp=False, g=g,
                                      gx=(c0 + 128 * pr) // 128))
            if has_s:
                for g in range(2):
                    units.append(dict(c=T, nq=TS, kt=KTs, bkt=bKTs, va=VAs, bva=bVAs, vbA=0, kA=0, smp=True, g=g, gx=0))

            def emit_qk(ui):
                u = units[ui]
                c, nq, g, kt, bkt = u["c"], u["nq"], u["g"], u["kt"], u["bkt"]
                sa, bsa = SP[2 * (ui % 2)], bSP[2 * (ui % 2)]
                sbk, bsb = SP[2 * (ui % 2) + 1], bSP[2 * (ui % 2) + 1]
                for hh in range(4):
                    h = 4 * g + hh
                    lo = 64 * (h % 2)
                    qb = h // 2
                    for blk, (sps, bsp, nk, k0) in enumerate(((sa, bsa, 128, u["kA"]), (sbk, bsb, nq, u["kA"] + 128))):
                        Top(R.matmul(sps[0:nk, hh * nq:(hh + 1) * nq], lhsT=kt[lo:lo + 64, g, k0:k0 + nk], rhs=QT[lo:lo + 64, qb, c:c + nq],
                                     start=True, stop=False), [bkt, bQT[qb]], [bsp])
                        Top(R.matmul(sps[0:nk, hh * nq:(hh + 1) * nq], lhsT=bias[0:nq, blk, h, 0:nk], rhs=IDB[0:nq, 0:nq],
                                     start=False, stop=True), [bC], [bsp])

            def emit_a(ui):
                u = units[ui]
                c, nq, g, va, bva = u["c"], u["nq"], u["g"], u["va"], u["bva"]
                s0 = 2 * (ui % 2)
                par = ui % 3
                gx = u["gx"]
                for blk, nk in ((0, 128), (1, nq)):
                    sps, bsp = SP[s0 + blk], bSP[s0 + blk]
                    if u["smp"]:
                        Aop(R.activation(out=PB[0:nk, s0 + blk, 0:4 * nq], in_=sps[0:nk, 0:4 * nq], func=AF.Exp),
                            [bsp], [bPB[s0 + blk]])
                    else:
                        Aop(R.activation(out=PB[0:nk, s0 + blk, 0:4 * nq], in_=sps[0:nk, 0:4 * nq], func=AF.Exp,
                                         bias=KM[0:nk, gx + blk:gx + blk + 1]), [bsp, bC], [bPB[s0 + blk]])
                po, bpo = pr_next()
                for hh in range(4):
                    for blk, nk in ((0, 128), (1, nq)):
                        vb = u["vbA"] + blk
                        Top(R.matmul(po[0:nq, hh * 65:(hh + 1) * 65], lhsT=PB[0:nk, s0 + blk, hh * nq:(hh + 1) * nq],
                                     rhs=va[0:nk, vb, g, 0:65], start=(blk == 0), stop=(blk == 1)),
                            [bPB[s0 + blk], bva], [bpo])
                o3 = po[:, 0:260].rearrange("p (h d) -> p h d", h=4)
                sk = l * 8 + 4 * g
                den = DEN[:, par, :]
                Vop(R.tensor_tensor(out=den[0:nq, 0:4].unsqueeze(2), in0=o3[0:nq, :, 64:65], in1=ESINK[0:nq, sk:sk + 4].unsqueeze(2), op=ALU.add),
                    [bpo, bC], [bDEN2[par]])
                Vop(R.reciprocal(out=den[0:nq, 0:4], in_=den[0:nq, 0:4]), [bDEN2[par]], [bDEN2[par]])
                Vop(R.tensor_tensor(out=ATOK[0:nq, par, :, :], in0=o3[0:nq, :, 0:64], in1=den[0:nq, 0:4].unsqueeze(2).to_broadcast([nq, 4, 64]), op=ALU.mult),
                    [bpo, bDEN2[par]], [bATOK2[par]])

            def emit_b(ui):
                u = units[ui]
                c, nq, g = u["c"], u["nq"], u["g"]
                par = ui % 3
                ps, bps = pr_next()
                for j in range(2):
                    Top(R.transpose(ps[:, j * 128:j * 128 + nq], ATOK[0:nq, par, 2 * j:2 * j + 2, :].rearrange("p a b -> p (a b)"), IDENT[0:nq, 0:nq]),
                        [bATOK2[par], bC], [bps])
                Vop(R.tensor_tensor(out=AT[:, 2 * g:2 * g + 2, c:c + nq], in0=ps[:, 0:256].rearrange("p (j q) -> p j q", j=2)[:, :, 0:nq],
                                    in1=SG[:, 2 * g:2 * g + 2, c:c + nq], op=ALU.mult),
                    [bps, bSG[2 * g], bSG[2 * g + 1]], [bAT[2 * g], bAT[2 * g + 1]])

            pr_wide[0] = False
            nun = len(units)
            emit_qk(0)
            if nun > 1:
                emit_qk(1)
            for ui in range(nun):
                emit_a(ui)
                if ui + 2 < nun:
                    emit_qk(ui + 2)
                if ui >= 2:
                    emit_b(ui - 2)
            if nun >= 2:
                emit_b(nun - 2)
            emit_b(nun - 1)
            if has_p:
                Aop(R.activation(out=KT[:, :, 0:128], in_=KT[:, :, T:T + 128], func=AF.Copy), [bKT], [bKT])
                Aop(R.activation(out=VA[:, 0, :, 0:64], in_=VA[:, T // 128, :, 0:64], func=AF.Copy), [bVA], [bVA])

            pr_wide[0] = True
            if nxt is not None:
                norm_part2(nxt[0], nxt[1], k + 1)

            merge_branch(l, 3, 4, 5, W, True, False, XN, bXN)

            cw = gcol + 16
            cbc = gcol + 28
            csegs = []
            if has_p:
                csegs.append((0, T, 0))
            if has_s:
                csegs.append((T, E, T + 2))
            for j in range(4):
                ij, wj, bwj = w_acquire(l, 6 + j)
                uj, buj = UJ[:, j % 2, :], bUJ[j % 2]
                ps, bps = proj(wj, bwj, w8(wj, 0), xr, [bXN], W)
                cc, bcc = ft_next()
                Aop(R.activation(out=cc[:, 0:W], in_=ps[:, 0:W], func=AF.Copy), [bps], [bcc])
                if has_p:
                    Aop(R.activation(out=uj[:, 0:2], in_=ULB[:, j, :], func=AF.Copy), [bULB], [buj])
                if has_s:
                    Aop(R.activation(out=uj[:, T + 2:T + 4], in_=ULBs[:, j, :], func=AF.Copy), [bULBs], [buj])
                ps, bps = proj(wj, bwj, w8(wj, 1), xr, [bXN], W)
                for (lo, n, uo) in csegs:
                    Vop(R.tensor_tensor(out=uj[:, uo + 2:uo + 2 + n], in0=cc[:, lo:lo + n], in1=ps[:, lo:lo + n], op=ALU.mult), [bcc, bps], [buj])
                cv, bcv = ft_next()
                for (lo, n, uo) in csegs:
                    Aop(R.activation(out=cv[:, lo:lo + n], in_=uj[:, uo + 2:uo + 2 + n], func=AF.Identity, scale=PV[:, cw + 8 + j:cw + 9 + j],
                                     bias=PV[:, cbc + j:cbc + j + 1]), [buj, bC], [bcv])
                for tap, pc in ((1, cw + 4 + j), (0, cw + j)):
                    for (lo, n, uo) in csegs:
                        Vop(R.scalar_tensor_tensor(out=cv[:, lo:lo + n], in0=uj[:, uo + tap:uo + tap + n], scalar=PV[:, pc:pc + 1], in1=cv[:, lo:lo + n],
                                                   op0=ALU.mult, op1=ALU.add), [buj, bcv, bC], [bcv])
                if has_p:
                    Aop(R.activation(out=ULB[:, j, :], in_=uj[:, T:T + 2], func=AF.Copy), [buj], [bULB])
                if has_s:
                    Aop(R.activation(out=ULBs[:, j, :], in_=uj[:, T + 2 + E:T + 4 + E], func=AF.Copy), [buj], [bULBs])
                ps, bps = proj(wj, bwj, w8(wj, 2), xr, [bXN], W)
                Vop(R.tensor_tensor(out=cv[:, 0:W], in0=cv[:, 0:W], in1=ps[:, 0:W], op=ALU.mult), [bcv, bps], [bcv])
                ps, bps = proj(wj, bwj, w8(wj, 3), xr, [bXN], W)
                tg, btg = ft_next()
                Aop(R.activation(out=tg[:, 0:W], in_=ps[:, 0:W], func=AF.Tanh, scale=0.5), [bps], [btg])
                Vop(R.scalar_tensor_tensor(out=tg[:, 0:W], in0=tg[:, 0:W], scalar=1.0, in1=ps[:, 0:W], op0=ALU.add, op1=ALU.mult),
                    [btg, bps], [btg])
                Vop(R.tensor_tensor(out=AT[:, j, 0:W], in0=cv[:, 0:W], in1=tg[:, 0:W], op=ALU.mult), [bcv, btg], [bAT[j]])
                w_release(ij)
            for (flag, ulb, bulb, dst) in ((has_s, ULBs, bULBs, cso[l]), (t == 4, ULB, bULB, co[l])):
                if flag:
                    ps, bps = pr_next()
                    for j in range(4):
                        Top(R.transpose(ps[0:2, j * 128:(j + 1) * 128], ulb[:, j, :], IDENT[:]), [bulb, bC], [bps])
                    out_rows(ps[0:2, 0:512], bps, 2, 512, dst)

            merge_branch(l, 10, 11, 12, W, False, False, XN, bXN)
            if l == 0 and t == 0:
                setup_mem()
                layer_setup(0)

            ix, wx, bwx = w_acquire(l, 13)
            for j in range(4):
                ps, bps = proj(wx, bwx, w8(wx, j), xr, [bXN], W)
                Aop(R.activation(out=QT[:, j, 0:W], in_=ps[:, 0:W], func=AF.Identity, scale=128.0 ** -0.5), [bps], [bQT[j]])
            w_release(ix)
            ig, wg, bwg = w_acquire(l, 14)
            for j in range(4):
                ps, bps = proj(wg, bwg, w8(wg, j), xr, [bXN], W)
                tanh_gate(ps, bps, W, SG[:, j, 0:W], bSG[j])
            w_release(ig)
            pr_wide[0] = False
            xunits = [(i * 128, 128, MVA, bMVA) for i in range(T // 128)]
            if has_s:
                xunits.append((T, TS, MVAs, bMVAs))
            xcnt = [0]
            xpend = []

            def x_scores(hp):
                for hh in range(2):
                    h = 2 * hp + hh
                    for blk in range(2):
                        kk = 2 * hh + blk
                        if has_p:
                            Top(R.matmul(SP[kk][:, 0:T], lhsT=MKT[:, h, blk * 128:(blk + 1) * 128], rhs=QT[:, h, 0:T], start=True, stop=True),
                                [bMKT, bQT[h]], [bSP[kk]])
                        if has_s:
                            Top(R.matmul(SP[kk][:, T:W], lhsT=MKTs[:, h, blk * 128:(blk + 1) * 128], rhs=QT[:, h, T:W], start=True, stop=True),
                                [bMKTs, bQT[h]], [bSP[kk]])
                        Aop(R.activation(out=PB[:, kk, 0:W], in_=SP[kk][:, 0:W], func=AF.Exp), [bSP[kk]], [bPB[kk]])

            def x_a(hp, xu):
                col, nt, mva, bmva = xu
                par = xcnt[0] % 3
                xcnt[0] += 1
                po, bpo = pr_next()
                for hh in range(2):
                    h = 2 * hp + hh
                    for blk in range(2):
                        kk = 2 * hh + blk
                        Top(R.matmul(po[0:nt, hh * 129:(hh + 1) * 129], lhsT=PB[:, kk, col:col + nt],
                                     rhs=mva[:, blk, h, 0:129], start=(blk == 0), stop=(blk == 1)), [bPB[kk], bmva], [bpo])
                o2 = po[:, 0:258].rearrange("p (h d) -> p h d", h=2)
                den = DEN[:, par, :]
                Vop(R.reciprocal(out=den[0:nt, 0:2].unsqueeze(2), in_=o2[0:nt, :, 128:129]), [bpo], [bDEN2[par]])
                Vop(R.tensor_tensor(out=XTOK[0:nt, par, :, :], in0=o2[0:nt, :, 0:128], in1=den[0:nt, 0:2].unsqueeze(2).to_broadcast([nt, 2, 128]), op=ALU.mult),
                    [bpo, bDEN2[par]], [bXTOK2[par]])
                return (hp, col, nt, par)

            def x_b(st):
                hp, col, nt, par = st
                ps, bps = pr_next()
                for hh in range(2):
                    Top(R.transpose(ps[:, hh * 128:hh * 128 + nt], XTOK[0:nt, par, hh, :], IDENT[0:nt, 0:nt]), [bXTOK2[par], bC], [bps])
                Vop(R.tensor_tensor(out=AT[:, 2 * hp:2 * hp + 2, col:col + nt], in0=ps[:, 0:256].rearrange("p (j q) -> p j q", j=2)[:, :, 0:nt],
                                    in1=SG[:, 2 * hp:2 * hp + 2, col:col + nt], op=ALU.mult),
                    [bps, bSG[2 * hp], bSG[2 * hp + 1]], [bAT[2 * hp], bAT[2 * hp + 1]])

            for hp in range(2):
                x_scores(hp)
                for xu in xunits:
                    xpend.append(x_a(hp, xu))
                    if len(xpend) > 2:
                        x_b(xpend.pop(0))
            while xpend:
                x_b(xpend.pop(0))

            pr_wide[0] = True
            if t == tiles_of(l)[-1] and l + 1 < L:
                layer_setup(l + 1)
            merge_branch(l, 15, 16, 17, W, False, True, XN, bXN)

            for half in range(2):
                io_, wo, bwo = w_acquire(l, 18 + half)
                for o4 in range(4):
                    ob = half * 4 + o4
                    ps, bps = proj(wo, bwo, w8(wo, o4), lambda kb: MACC[:, kb, 0:W], bMACC, W)
                    for (lo, n, xc, bx) in segs:
                        Vop(R.scalar_tensor_tensor(out=X[:, ob, xc:xc + n], in0=ps[:, lo:lo + n], scalar=0.25, in1=X[:, ob, xc:xc + n],
                                                   op0=ALU.mult, op1=ALU.add), [bps, bx], [bx])
                w_release(io_)

        yc = [0]

        def final_tiles(which):
            fg = 128
            pr_wide[0] = True
            tiles = []
            for q in which:
                if q == 5:
                    tiles.append((5, NP, TS, y_smp))
                else:
                    tiles.append((q, HALO + 512 * (q - 1), 512, y_own[512 * (q - 1):512 * q, :]))
            for (t, c0, T, dst) in tiles:
                ps, bps = pr_next()
                for kb in range(8):
                    sq, bsq = FT[:, 1 + kb % 2, :].bitcast(BF16), bFT[1 + kb % 2]
                    Aop(R.activation(out=sq[:, 0:T], in_=X[:, kb, c0:c0 + T], func=AF.Square), [bX[t]], [bsq])
                    Top(R.matmul(ps[:, 0:T], lhsT=ONESB[:], rhs=sq[:, 0:T], start=(kb == 0), stop=(kb == 7)), [bsq, bC], [bps])
                ms, bms = FT[:, 0, :], bFT[0]
                Aop(R.activation(out=ms[:, 0:T], in_=ps[:, 0:T], func=AF.Sqrt, scale=1.0 / D, bias=EPSC[:, 0:1]), [bps, bC], [bms])
                Vop(R.reciprocal(out=ms[:, 0:T], in_=ms[:, 0:T]), [bms], [bms])
                items = [(b0, min(128, T - b0), kb) for b0 in range(0, T, 128) for kb in range(8)]
                yts = {}
                cur = {}

                def f_s(idx):
                    b0, n, kb = items[idx]
                    sl = 1 + (yc[0] % 7)
                    yc[0] += 1
                    yt, byt = FT[:, sl, :], bFT[sl]
                    Vop(R.scalar_tensor_tensor(out=yt[:, 0:n], in0=X[:, kb, c0 + b0:c0 + b0 + n], scalar=PV[:, fg + kb:fg + kb + 1],
                                               in1=ms[:, b0:b0 + n], op0=ALU.mult, op1=ALU.mult), [bX[t], bms, bC], [byt])
                    yts[idx] = (yt, byt)

                def f_t(idx):
                    b0, n, kb = items[idx]
                    hf, k4 = divmod(kb, 4)
                    if kb == 0:
                        cur["io"] = io_next()
                    if k4 == 0:
                        cur["ps"] = pr_next()
                    io, bio = cur["io"]
                    ps2, bps2 = cur["ps"]
                    yt, byt = yts.pop(idx)
                    Top(R.transpose(ps2[0:n, k4 * 128:(k4 + 1) * 128], yt[:, 0:n], IDENT[:]), [byt, bC], [bps2])
                    if k4 == 3:
                        if hf == 0:
                            Aop(R.activation(out=io[0:n, 0:512], in_=ps2[0:n, :], func=AF.Copy), [bps2], [bio])
                        else:
                            Vop(R.tensor_copy(out=io[0:n, 512:1024], in_=ps2[0:n, :]), [bps2], [bio])
                    if kb == 7:
                        Dma(R.dma_start(out=dst[b0:b0 + n, :], in_=io[0:n, :]), [bio], [], "D_" + bio.name)

                LOOK = 6
                for idx in range(min(LOOK, len(items))):
                    f_s(idx)
                for idx in range(len(items)):
                    f_t(idx)
                    if idx + LOOK < len(items):
                        f_s(idx + LOOK)

        setup()
        bC.const = True
        import os as _os
        _nsteps = int(_os.environ.get("MK_STEPS", "100000"))
        _k = 0
        for l in range(L):
            for t in tiles_of(l):
                tile_layer(l, t)
                if l == L - 1:
                    if t in (1, 2, 3, 4):
                        final_tiles([t])
                    if t == 5 or (t == 0 and L > 1 and RIDER):
                        final_tiles([5])
        P.wait_all("sp", [("d", n, 16 * c) for n, c in P.dma_count.items()])

        sems = {n: es.enter_context(nc.semaphore(n)) for n in P.sem_names()}
        with nc.Block() as block:
            @block.tensor
            def _(e):
                P.replay("pe", e, sems)

            @block.scalar
            def _(e):
                P.replay("act", e, sems)

            @block.vector
            def _(e):
                P.replay("dve", e, sems)

            @block.gpsimd
            def _(e):
                P.replay("pool", e, sems)

            @block.sync
            def _(e):
                P.replay("sp", e, sems)
    return nc


def _pack_weights(w_in, w_pa, w_pb, w_px, w_out, w_mk, w_mv, L):
    def k8(m):
        return m.reshape(8, 128, 512).transpose(1, 0, 2).reshape(128, 4096)

    def k4(m):
        return m.reshape(4, 128, 1024).transpose(1, 0, 2).reshape(128, 4096)

    out = np.empty((L, NCH, 128, 4096), np.float32)
    r = np.arange
    for l in range(L):
        wi = w_in[l]
        cols = {}
        cols[0] = r(0, 512)
        cols[1] = np.concatenate([r(512, 576), r(512, 576), r(576, 640), r(576, 640), r(640, 768), r(640, 768)])
        cols[2] = r(768, 1280)
        cols[3] = r(4352, 4864)
        cols[4] = r(4864, 5376)
        for j in range(4):
            cols[6 + j] = np.concatenate([r(1792 + 128 * j, 1920 + 128 * j), r(2304 + 128 * j, 2432 + 128 * j),
                                          r(1280 + 128 * j, 1408 + 128 * j), r(2816 + 128 * j, 2944 + 128 * j)])
        cols[10] = r(5376, 5888)
        cols[11] = r(5888, 6400)
        cols[13] = r(3328, 3840)
        cols[14] = r(3840, 4352)
        cols[15] = r(6400, 6912)
        cols[16] = r(6912, 7424)
        for c, idx in cols.items():
            out[l, c] = k8(wi[:, idx])
        out[l, 5] = k4(w_pa[l])
        out[l, 12] = k4(w_pb[l])
        out[l, 17] = k4(w_px[l])
        out[l, 18] = k8(w_out[l][:, 0:512])
        out[l, 19] = k8(w_out[l][:, 512:1024])
        out[l, 20] = k8(w_mk[l])
        out[l, 21] = k8(w_mv[l])
    return out


def _bias_tables():
    slopes = 2.0 ** (-8.0 * np.arange(1, 9) / 8.0)
    bp = np.zeros((128, 2, 8, 128), np.float32)
    qi = np.arange(128)[:, None]
    kk = np.arange(128)[None, :]
    for blk in range(2):
        kpos = kk - 128 if blk == 0 else kk
        dist = np.abs(qi - kpos).astype(np.float32)
        qc = qi // 64
        kc = np.floor_divide(kpos, 64)
        vis = (kc >= qc - 2) & (kc <= qc)
        for h in range(8):
            bp[:, blk, h, :] = np.where(vis, -slopes[h] * dist, NEG)
    bs = np.zeros((128, 2, 8, 128), np.float32)
    for blk in range(2):
        kpos = kk - 128 if blk == 0 else kk
        dist = np.abs(qi - kpos).astype(np.float32)
        for h in range(8):
            bs[:, blk, h, :] = -slopes[h] * dist
    return (bp.reshape(128, -1).astype(ml_dtypes.bfloat16), bs.reshape(128, -1).astype(ml_dtypes.bfloat16))


_PROG = {}


def make_in_maps(x_prompt, x_sample, cache_attn_k, cache_attn_v, cache_conv, cache_mem_k, cache_mem_v,
                 mem_prompt, norm_g, w_in, attn_sink, w_pa, conv_w, conv_b, w_pb, mem_norm_g,
                 w_mk, w_mv, w_px, w_out, final_g, L=DEPTH):
    f = lambda a: np.ascontiguousarray(np.asarray(a, dtype=np.float32))
    x_prompt, x_sample = f(x_prompt), f(x_sample)
    wpack = _pack_weights(f(w_in), f(w_pa), f(w_pb), f(w_px), f(w_out), f(w_mk), f(w_mv), L)
    bp, bs = _bias_tables()
    pvec = np.zeros((136, 128), np.float32)
    ng, mg, cwt, cbs = f(norm_g), f(mem_norm_g), f(conv_w), f(conv_b)
    for l in range(L):
        pvec[l * 32:l * 32 + 8] = ng[l].reshape(8, 128)
        pvec[l * 32 + 8:l * 32 + 16] = mg[l].reshape(8, 128)
        pvec[l * 32 + 16:l * 32 + 28] = cwt[l].reshape(3, 4, 128).reshape(12, 128)
        pvec[l * 32 + 28:l * 32 + 32] = cbs[l].reshape(4, 128)
    pvec[128:136] = f(final_g).reshape(8, 128)
    sinkb = np.zeros((128, 32), np.float32)
    sinkb[:, :L * 8] = np.broadcast_to(f(attn_sink)[:L].reshape(1, L * 8), (128, L * 8))
    ident = np.eye(128, dtype=np.float32)
    identb = ident.astype(ml_dtypes.bfloat16)
    ck, cv, cc_, cmk, cmv, memp = f(cache_attn_k), f(cache_attn_v), f(cache_conv), f(cache_mem_k), f(cache_mem_v), f(mem_prompt)

    in_maps = []
    for i in range(NCORE):
        b, j = divmod(i, 4)
        xin = np.zeros((NTOK, D), np.float32)
        a = j * OWN
        if j > 0:
            xin[0:HALO] = x_prompt[b, a - HALO:a]
        xin[HALO:NP] = x_prompt[b, a:a + OWN]
        xin[NP:] = x_sample[i]
        km = np.zeros((128, 22), np.float32)
        km[:, 0] = NEG
        if j == 0:
            km[:, 1:5] = NEG
        in_maps.append(dict(
            xin=xin, wpack=wpack, pvec=pvec, sinkb=sinkb, kmask=km, ident=ident, identb=identb, biasp=bp,
            memp=memp[b],
            ck=np.ascontiguousarray(ck[:L, i].reshape(L, 128, 128)), cv=np.ascontiguousarray(cv[:L, i].reshape(L, 128, 128)),
            cconv=np.ascontiguousarray(cc_[:L, i]),
            cmk=np.ascontiguousarray(cmk[:L, i].reshape(L, 256, 512)), cmv=np.ascontiguousarray(cmv[:L, i].reshape(L, 256, 512)),
        ))
    return in_maps


def kernel(x_prompt, x_sample, cache_attn_k, cache_attn_v, cache_conv, cache_mem_k, cache_mem_v,
           mem_prompt, norm_g, w_in, attn_sink, w_pa, conv_w, conv_b, w_pb, mem_norm_g,
           w_mk, w_mv, w_px, w_out, final_g, _n_layers=DEPTH):
    L = _n_layers
    in_maps = make_in_maps(x_prompt, x_sample, cache_attn_k, cache_attn_v, cache_conv, cache_mem_k, cache_mem_v,
                           mem_prompt, norm_g, w_in, attn_sink, w_pa, conv_w, conv_b, w_pb, mem_norm_g,
                           w_mk, w_mv, w_px, w_out, final_g, L=L)
    if L not in _PROG:
        _PROG[L] = build_program(L)
    res = run_bass_kernel_spmd(_PROG[L], in_maps, core_ids=list(range(NCORE)))
    R = res.results
    y_prompt = np.stack([np.concatenate([R[b * 4 + j]["y_own"] for j in range(4)], axis=0) for b in range(2)])
    y_sample = np.stack([R[i]["y_smp"] for i in range(NCORE)])
    pk = np.stack([R[3]["ko"], R[7]["ko"]], axis=1).reshape(L, 2, 128, 2, 64)
    pv = np.stack([R[3]["vo"], R[7]["vo"]], axis=1).reshape(L, 2, 128, 2, 64)
    pc = np.stack([R[3]["co"], R[7]["co"]], axis=1).reshape(L, 2, 2, 512)
    pmk = np.stack([R[0]["mko"], R[4]["mko"]], axis=1).reshape(L, 2, 256, 4, 128)
    pmv = np.stack([R[0]["mvo"], R[4]["mvo"]], axis=1).reshape(L, 2, 256, 4, 128)
    sk = np.stack([R[i]["kso"] for i in range(NCORE)], axis=1).reshape(L, NCORE, TS, 2, 64)
    sv = np.stack([R[i]["vso"] for i in range(NCORE)], axis=1).reshape(L, NCORE, TS, 2, 64)
    sc = np.stack([R[i]["cso"] for i in range(NCORE)], axis=1).reshape(L, NCORE, 2, 512)
    outs = (y_prompt, y_sample, pk, pv, pc, pmk, pmv, sk, sv, sc)
    return tuple(np.ascontiguousarray(o.astype(np.float32)) for o in outs)
```
